# Optimizing a Trainium2 kernel written in Bass

```python
import math
import jax
import jax.numpy as jnp
from jax import lax
import numpy as np

D_MODEL = 2048
BATCH = 4
SEQ = 2048
DEPTH = 4
DEC_BATCH = 8
DEC_SEQ = 4
PAST_LEN = 16384
PAGE_SIZE = 128

HEAD_DIM = 128
H_PER_GROUP = 8
ATT_GROUPS = ((128, 1), (512, 4), (2048, 16))
N_GROUPS = 3
N_ATT_HEADS = N_GROUPS * H_PER_GROUP
QKV_WIDTH = N_ATT_HEADS * HEAD_DIM
ATT_WIDTH = H_PER_GROUP * HEAD_DIM
N_BUCKETS = 32
REL_MAX_DIST = 2048
BLK = 128
LRU_WIDTH = D_MODEL
LRU_BLOCKS = 16
LRU_BLOCK = LRU_WIDTH // LRU_BLOCKS
CONV_WIDTH = 4
LRU_C = 8.0
IN_COLS = 3 * QKV_WIDTH + ATT_WIDTH + 2 * LRU_WIDTH + 2 * D_MODEL
RMS_EPS = 1e-6
NEG = -1e30

kernel_name = "hybrid_dilated_attn_rglru_decoder_step"


def _rmsnorm(x, g):
    xf = x.astype(jnp.float32)
    y = xf * lax.rsqrt(jnp.mean(xf * xf, axis=-1, keepdims=True) + RMS_EPS)
    return (y * g.astype(jnp.float32)).astype(x.dtype)


def _t5_bucket(dist):
    dist = np.asarray(dist).astype(np.int32)
    max_exact = N_BUCKETS // 2
    safe = np.maximum(dist, 1).astype(np.float32)
    large = max_exact + (np.log(safe / max_exact) / np.float32(math.log(REL_MAX_DIST / max_exact))
                         * (N_BUCKETS - max_exact)).astype(np.int32)
    large = np.minimum(large, N_BUCKETS - 1)
    return np.where(dist < max_exact, dist, large).astype(np.int32)


def _band_attn_prompt(q, k, v, tbl, window, dil):
    B, T, H, Dh = q.shape
    J = window // dil
    span = BLK * dil
    Tp = -(-T // span) * span
    NB = Tp // span

    def blocks(a):
        a = jnp.pad(a, ((0, 0), (0, Tp - T), (0, 0), (0, 0)))
        a = a.reshape(B, NB, BLK, dil, H, Dh)
        return a.transpose(0, 3, 4, 1, 2, 5)

    def with_prev(a):
        prev = jnp.pad(a, ((0, 0), (0, 0), (0, 0), (1, 0), (0, 0), (0, 0)))[:, :, :, :-1]
        return jnp.concatenate([prev, a], axis=4)

    qb = blocks(q)
    kk = with_prev(blocks(k))
    vv = with_prev(blocks(v))
    qi = np.arange(BLK)[:, None]
    ki = np.arange(2 * BLK)[None, :]
    du = qi + BLK - ki
    band = (du >= 0) & (du <= J)
    nb = np.arange(NB)[:, None, None]
    valid = band[None] & (nb * BLK + ki[None] - BLK >= 0)
    bias = tbl[_t5_bucket(np.clip(du, 0, None) * dil)]
    bias = jnp.transpose(bias, (2, 0, 1)).astype(jnp.float32)
    logits = jnp.einsum('brhnqe,brhnke->brhnqk', qb, kk,
                        preferred_element_type=jnp.float32) * (HEAD_DIM ** -0.5)
    logits = jnp.where(valid[None, None, None], logits + bias[None, None, :, None], NEG)
    m = logits.max(axis=-1)
    p = jnp.exp(logits - m[..., None])
    s = p.sum(axis=-1)
    acc = jnp.einsum('brhnqk,brhnke->brhnqe', p, vv.astype(jnp.float32))
    acc = acc.transpose(0, 3, 4, 1, 2, 5).reshape(B, Tp, H, Dh)[:, :T]
    m = m.transpose(0, 3, 4, 1, 2).reshape(B, Tp, H)[:, :T]
    s = s.transpose(0, 3, 4, 1, 2).reshape(B, Tp, H)[:, :T]
    return acc, m, s


def _dilated_attn_step(q, k, v, kv_buf, tbl, window, dil):
    S = q.shape[1]
    Wb = kv_buf.shape[1]
    J = window // dil
    ext = jnp.concatenate([kv_buf, jnp.stack([k, v], axis=2).astype(kv_buf.dtype)], axis=1)
    j = np.arange(J + 1)
    idx = Wb + np.arange(S)[:, None] - j[None, :] * dil
    valid = idx >= 0
    g = ext[:, np.clip(idx, 0, None)]
    bias = tbl[_t5_bucket(j * dil)].astype(jnp.float32)
    logits = jnp.einsum('bshe,bsjhe->bshj', q, g[:, :, :, 0],
                        preferred_element_type=jnp.float32) * (HEAD_DIM ** -0.5)
    logits = jnp.where(valid[None, :, None, :], logits + bias.T[None, None], NEG)
    m = logits.max(axis=-1)
    p = jnp.exp(logits - m[..., None])
    s = p.sum(axis=-1)
    acc = jnp.einsum('bshj,bsjhe->bshe', p, g[:, :, :, 1].astype(jnp.float32))
    return acc, m, s, ext[:, S:]


def _merge_groups(accs, ms, ss):
    acc = jnp.stack(accs)
    m = jnp.stack(ms)
    s = jnp.stack(ss)
    w = jnp.exp(m - m.max(axis=0, keepdims=True))
    return (w[..., None] * acc).sum(axis=0) / (w * s).sum(axis=0)[..., None]


def _causal_conv(xb, buf, w, b):
    T = xb.shape[1]
    xp = jnp.concatenate([buf.astype(xb.dtype), xb], axis=1)
    y = xp[:, 0:T] * w[0]
    for tap in range(1, CONV_WIDTH):
        y = y + xp[:, tap:tap + T] * w[tap]
    return y + b, xp[:, T:]


def _rglru(x, h0, w_r, b_r, w_i, b_i, lam):
    B, T, C = x.shape
    xf = x.astype(jnp.float32)
    xblk = xf.reshape(B, T, LRU_BLOCKS, LRU_BLOCK)
    r = jax.nn.sigmoid(jnp.einsum('btnc,ncd->btnd', xblk, w_r.astype(jnp.float32)).reshape(B, T, C) + b_r)
    i = jax.nn.sigmoid(jnp.einsum('btnc,ncd->btnd', xblk, w_i.astype(jnp.float32)).reshape(B, T, C) + b_i)
    log_a = -LRU_C * r * jax.nn.softplus(-lam.astype(jnp.float32))
    a = jnp.exp(log_a)
    u = jnp.sqrt(-jnp.expm1(2.0 * log_a)) * (i * xf)
    u = u.at[:, 0].add(a[:, 0] * h0.astype(jnp.float32))

    def comb(l, rr):
        return (l[0] * rr[0], rr[0] * l[1] + rr[1])

    _, h = lax.associative_scan(comb, (a, u), axis=1)
    return h, h[:, -1]


def _layer(x, c, kv_bufs, conv_buf, h0, rel_bias, w_ada, b_ada, norm_g, w_in, conv_w, conv_b,
           w_r, b_r, w_i, b_i, lam, w_pa, w_pb, w_out):
    B, T, _ = x.shape
    mod = jnp.einsum('bd,de->be', jax.nn.silu(c), w_ada) + b_ada
    shift, scale, gate = jnp.split(mod, 3, axis=-1)
    h = _rmsnorm(x, norm_g) * (1 + scale[:, None]) + shift[:, None]
    proj = jnp.einsum('btd,de->bte', h, w_in)
    cuts = np.cumsum([QKV_WIDTH, QKV_WIDTH, QKV_WIDTH, ATT_WIDTH, LRU_WIDTH, LRU_WIDTH, D_MODEL]).tolist()
    q, k, v, z_att, x_lru, z_lru, g_att, g_lru = jnp.split(proj, cuts, axis=-1)
    heads = (B, T, N_GROUPS, H_PER_GROUP, HEAD_DIM)
    q, k, v = q.reshape(heads), k.reshape(heads), v.reshape(heads)
    accs, ms, ss, new_kv = [], [], [], []
    for gi, (win, dil) in enumerate(ATT_GROUPS):
        tbl = rel_bias[:, gi * H_PER_GROUP:(gi + 1) * H_PER_GROUP]
        qg, kg, vg = q[:, :, gi], k[:, :, gi], v[:, :, gi]
        if kv_bufs is None:
            acc, m, s = _band_attn_prompt(qg, kg, vg, tbl, win, dil)
            new = jnp.stack([kg, vg], axis=2)[:, T - min(win, T):]
        else:
            acc, m, s, new = _dilated_attn_step(qg, kg, vg, kv_bufs[gi], tbl, win, dil)
        accs.append(acc)
        ms.append(m)
        ss.append(s)
        new_kv.append(new)
    att = _merge_groups(accs, ms, ss).reshape(B, T, ATT_WIDTH).astype(x.dtype)
    y_att = jnp.einsum('bte,ed->btd', att * jax.nn.silu(z_att), w_pa)
    if conv_buf is None:
        conv_buf = jnp.zeros((B, CONV_WIDTH - 1, LRU_WIDTH), x.dtype)
        h0 = jnp.zeros((B, LRU_WIDTH), jnp.float32)
    xc, new_conv = _causal_conv(x_lru, conv_buf, conv_w, conv_b)
    hs, h_last = _rglru(xc, h0, w_r, b_r, w_i, b_i, lam)
    y_lru = jnp.einsum('bte,ed->btd', hs.astype(x.dtype) * jax.nn.silu(z_lru), w_pb)
    merged = jax.nn.sigmoid(g_att) * y_att + jax.nn.sigmoid(g_lru) * y_lru
    out = jnp.einsum('btd,de->bte', merged, w_out)
    return x + gate[:, None] * out, new_kv, new_conv, h_last


def setup_inputs(seed: int = 0) -> dict:
    key = jax.random.key(seed)
    ks = jax.random.split(key, 26)
    f32 = jnp.float32

    def nrm(k, shape, s):
        return jax.random.normal(k, shape, f32) * s

    (w0, _), (w1, _), (w2, _) = ATT_GROUPS
    u = jax.random.uniform(ks[20], (DEPTH, LRU_WIDTH), f32, 0.9, 0.999)
    sig = u ** (1.0 / LRU_C)
    lam = jnp.log(sig) - jnp.log1p(-sig)
    return {
        'x_prompt': nrm(ks[0], (BATCH, SEQ, D_MODEL), 1.0),
        'x_sample': nrm(ks[1], (DEC_BATCH, DEC_SEQ, D_MODEL), 1.0),
        'c_prompt': nrm(ks[2], (BATCH, D_MODEL), 1.0),
        'c_sample': nrm(ks[3], (DEC_BATCH, D_MODEL), 1.0),
        'cache_kv_g0': nrm(ks[4], (DEPTH, DEC_BATCH, min(w0, PAST_LEN), 2, H_PER_GROUP, HEAD_DIM), 1.0),
        'cache_kv_g1': nrm(ks[5], (DEPTH, DEC_BATCH, min(w1, PAST_LEN), 2, H_PER_GROUP, HEAD_DIM), 1.0),
        'cache_kv_g2': nrm(ks[6], (DEPTH, DEC_BATCH, min(w2, PAST_LEN), 2, H_PER_GROUP, HEAD_DIM), 1.0),
        'state_conv': nrm(ks[7], (DEPTH, DEC_BATCH, CONV_WIDTH - 1, LRU_WIDTH), 1.0),
        'state_h': nrm(ks[8], (DEPTH, DEC_BATCH, LRU_WIDTH), 0.5),
        'rel_bias': nrm(ks[9], (N_BUCKETS, N_ATT_HEADS), 0.3),
        'w_ada': nrm(ks[10], (DEPTH, D_MODEL, 3 * D_MODEL), 0.5 * D_MODEL ** -0.5),
        'b_ada': nrm(ks[11], (DEPTH, 3 * D_MODEL), 0.02),
        'norm_g': 1.0 + nrm(ks[12], (DEPTH, D_MODEL), 0.02),
        'w_in': nrm(ks[13], (DEPTH, D_MODEL, IN_COLS), D_MODEL ** -0.5),
        'conv_w': nrm(ks[14], (DEPTH, CONV_WIDTH, LRU_WIDTH), CONV_WIDTH ** -0.5),
        'conv_b': nrm(ks[15], (DEPTH, LRU_WIDTH), 0.02),
        'w_r': nrm(ks[16], (DEPTH, LRU_BLOCKS, LRU_BLOCK, LRU_BLOCK), LRU_BLOCK ** -0.5),
        'b_r': nrm(ks[17], (DEPTH, LRU_WIDTH), 0.1),
        'w_i': nrm(ks[18], (DEPTH, LRU_BLOCKS, LRU_BLOCK, LRU_BLOCK), LRU_BLOCK ** -0.5),
        'b_i': nrm(ks[19], (DEPTH, LRU_WIDTH), 0.1),
        'lam': lam,
        'w_pa': nrm(ks[21], (DEPTH, ATT_WIDTH, D_MODEL), ATT_WIDTH ** -0.5),
        'w_pb': nrm(ks[22], (DEPTH, LRU_WIDTH, D_MODEL), LRU_WIDTH ** -0.5),
        'w_out': nrm(ks[23], (DEPTH, D_MODEL, D_MODEL), D_MODEL ** -0.5),
        'final_g': 1.0 + nrm(ks[24], (D_MODEL,), 0.02),
    }


def reference(x_prompt, x_sample, c_prompt, c_sample, cache_kv_g0, cache_kv_g1, cache_kv_g2,
              state_conv, state_h, rel_bias, w_ada, b_ada, norm_g, w_in, conv_w, conv_b,
              w_r, b_r, w_i, b_i, lam, w_pa, w_pb, w_out, final_g):
    kv_p = ([], [], [])
    kv_s = ([], [], [])
    conv_p, conv_s, h_p, h_s = [], [], [], []
    xp, xs = x_prompt, x_sample
    for l in range(DEPTH):
        lw = (w_ada[l], b_ada[l], norm_g[l], w_in[l], conv_w[l], conv_b[l], w_r[l], b_r[l],
              w_i[l], b_i[l], lam[l], w_pa[l], w_pb[l], w_out[l])
        xp, kv, cv, hl = _layer(xp, c_prompt, None, None, None, rel_bias, *lw)
        for gi in range(N_GROUPS):
            kv_p[gi].append(kv[gi])
        conv_p.append(cv)
        h_p.append(hl)
        xs, kv, cv, hl = _layer(xs, c_sample, (cache_kv_g0[l], cache_kv_g1[l], cache_kv_g2[l]),
                                state_conv[l], state_h[l], rel_bias, *lw)
        for gi in range(N_GROUPS):
            kv_s[gi].append(kv[gi])
        conv_s.append(cv)
        h_s.append(hl)
    y_prompt = _rmsnorm(xp, final_g)
    y_sample = _rmsnorm(xs, final_g)
    kv_g0_prompt = jnp.stack(kv_p[0])
    kv_g1_prompt = jnp.stack(kv_p[1])
    kv_g2_prompt = jnp.stack(kv_p[2])
    conv_prompt = jnp.stack(conv_p)
    h_prompt = jnp.stack(h_p)
    kv_g0_sample = jnp.stack(kv_s[0])
    kv_g1_sample = jnp.stack(kv_s[1])
    kv_g2_sample = jnp.stack(kv_s[2])
    conv_sample = jnp.stack(conv_s)
    h_sample = jnp.stack(h_s)
    return (y_prompt, y_sample, kv_g0_prompt, kv_g1_prompt, kv_g2_prompt, conv_prompt, h_prompt,
            kv_g0_sample, kv_g1_sample, kv_g2_sample, conv_sample, h_sample)
```

```python
import contextlib
import math
import numpy as np
import concourse.bass as bass
import concourse.mybir as mybir
from concourse.bass_utils import run_bass_kernel_spmd

F32 = mybir.dt.float32
BF16 = mybir.dt.bfloat16
AF = mybir.ActivationFunctionType
ALU = mybir.AluOpType

D = 2048
DEPTH = 4
T = 2048
TC = 1024
TS = 4
NTOK = T + TS
HD = 128
NB_BUCK = 32
IN_COLS = 18432
QO, KO, VO, ZAO, XLO, ZLO, GAO, GLO = 0, 3072, 6144, 9216, 10240, 12288, 14336, 16384
DIL = (1, 4, 16)
WB = (128, 512, 2048)
SCALE = HD ** -0.5
NEGM = -30000.0

_STEP_HOOK = None
ENGS = ("tensor", "vector", "scalar", "gpsimd", "sync")
SEM_EPOCH = 30000
NDMASEM = 6


def I(name, *a, **k):
    return (name, a, k)


class Sched:
    def __init__(self, nc, stack):
        self.nc = nc
        self.stack = stack
        self.nsem = 0
        self.lists = {e: [] for e in ENGS}
        self.cnt = {e: 0 for e in ENGS}
        self.sem = {e: self._newsem("s_" + e) for e in ENGS}
        self.waited = {e: {} for e in ENGS}
        self.lastw = {}
        self.readers = {}
        self.dsem = {e: [[self._newsem("d_%s%d" % (e, i)), 0] for i in range(NDMASEM)]
                     for e in ("sync", "gpsimd", "scalar")}
        self.drr = {e: 0 for e in ("sync", "gpsimd", "scalar")}

    def _newsem(self, name):
        self.nsem += 1
        return self.stack.enter_context(self.nc.semaphore(name + "_%d" % self.nsem))

    def _need(self, eng, ev, waits):
        if ev is None:
            return
        s, v = ev
        k = id(s)
        if self.waited[eng].get(k, 0) >= v:
            return
        self.waited[eng][k] = v
        waits.append((s, v))

    def _deps(self, eng, reads, writes):
        waits = []
        for b in reads:
            self._need(eng, self.lastw.get(b), waits)
        for b in writes:
            self._need(eng, self.lastw.get(b), waits)
            for ev in self.readers.get(b, ()):
                self._need(eng, ev, waits)
        return waits

    def _commit(self, ev, reads, writes):
        for b in reads:
            self.readers.setdefault(b, []).append(ev)
        for b in writes:
            self.lastw[b] = ev
            self.readers[b] = []

    def op(self, eng, fn, reads=(), writes=(), own_sync=True):
        pr = [b for b in reads if isinstance(b, tuple) and b[0] == "pb"]
        if pr:
            reads = [b for b in reads if not (isinstance(b, tuple) and b[0] == "pb")]
            writes = list(writes) + pr
        waits = self._deps(eng, reads, writes)
        if self.cnt[eng] >= SEM_EPOCH:
            self.sem[eng] = self._newsem("s_" + eng)
            self.cnt[eng] = 0
        self.cnt[eng] += 1
        ev = (self.sem[eng], self.cnt[eng])
        if not own_sync:
            waits = [w for w in waits if w[0] is not self.sem[eng]]
        self.lists[eng].append((waits, fn, (self.sem[eng], 1)))
        self._commit(ev, reads, writes)
        return ev

    def dma(self, eng, fn, reads=(), writes=()):
        waits = self._deps(eng, reads, writes)
        slot = self.dsem[eng][self.drr[eng] % NDMASEM]
        self.drr[eng] += 1
        s, v = slot
        if v > 0:
            self._need(eng, (s, v), waits)
        slot[1] = v + 16
        ev = (s, v + 16)
        self.lists[eng].append((waits, fn, (s, 16)))
        self._commit(ev, reads, writes)
        return ev

    def finish(self, final_events):
        waits = []
        for ev in final_events:
            self._need("sync", ev, waits)
        self.lists["sync"].append((waits, None, None))
        lists = self.lists

        def replay(e, items):
            for waits, fn, inc in items:
                for s, v in waits:
                    e.wait_ge(s, v)
                if fn is not None:
                    getattr(e, fn[0])(*fn[1], **fn[2]).then_inc(inc[0], inc[1])

        with self.nc.Block() as block:
            @block.tensor
            def _(e):
                replay(e, lists["tensor"])

            @block.vector
            def _(e):
                replay(e, lists["vector"])

            @block.scalar
            def _(e):
                replay(e, lists["scalar"])

            @block.gpsimd
            def _(e):
                replay(e, lists["gpsimd"])

            @block.sync
            def _(e):
                replay(e, lists["sync"])


def _t5_bucket(dist):
    dist = np.asarray(dist).astype(np.int32)
    max_exact = NB_BUCK // 2
    safe = np.maximum(dist, 1).astype(np.float32)
    large = max_exact + (np.log(safe / max_exact) / np.float32(math.log(2048 / max_exact))
                         * (NB_BUCK - max_exact)).astype(np.int32)
    large = np.minimum(large, NB_BUCK - 1)
    return np.where(dist < max_exact, dist, large).astype(np.int32)


def _bias_index_tables():
    P = -np.ones((3, 2, 128, 128), np.int64)
    k = np.arange(128)[:, None]
    q = np.arange(128)[None, :]
    for g in (0, 1):
        du = q - k
        P[g, 0] = np.where(du >= 0, _t5_bucket(np.clip(du, 0, None) * DIL[g]), -1)
        du = q + 128 - k
        P[g, 1] = np.where(du <= 128, _t5_bucket(du * DIL[g]), -1)
    rk, uk = k % 2, k // 2
    rq, uq = q % 2, q // 2
    du = uq - uk
    P[2, 0] = np.where((rk == rq) & (du >= 0), _t5_bucket(np.clip(du, 0, None) * 16), -1)
    du = uq + 64 - uk
    P[2, 1] = np.where(rk == rq, _t5_bucket(du * 16), -1)
    C = -np.ones((3, 4, 128), np.int64)
    N = -np.ones((3, 4, 4), np.int64)
    i = np.arange(128)
    for s in range(4):
        j = 128 + s - i
        C[0, s] = np.where(i >= s, _t5_bucket(np.clip(j, 0, 200)), -1)
        for g in (1, 2):
            C[g, s] = _t5_bucket((128 - i) * DIL[g])
        for kk in range(4):
            for g in range(3):
                dd = s - kk
                if dd >= 0 and dd % DIL[g] == 0:
                    N[g, kk, s] = _t5_bucket(dd)
    return P, C, N


def _gather_bias(rel_bias):
    P, C, N = _bias_index_tables()
    rb = np.concatenate([np.asarray(rel_bias, np.float32), np.full((1, 24), NEGM, np.float32)], 0)
    bP = np.zeros((4, 128, 3, 2, 2, 128), np.float32)
    bC = np.zeros((4, 128, 3, 4, 2), np.float32)
    bN = np.zeros((4, 4, 3, 2, 4), np.float32)
    for hp in range(4):
        for g in range(3):
            for hh in range(2):
                col = g * 8 + hp * 2 + hh
                for ty in range(2):
                    bP[hp, :, g, hh, ty, :] = rb[P[g, ty], col]
                for s in range(4):
                    bC[hp, :, g, s, hh] = rb[C[g, s], col]
                bN[hp, :, g, hh, :] = rb[N[g], col]
    return bP, bC, bN


def build(depth=DEPTH):
    nc = bass.Bass("TRN2", target_bir_lowering=False)

    def din(name, shape, dt=F32):
        return nc.dram_tensor(name, list(shape), dt, kind="ExternalInput").ap()

    def dout(name, shape, dt=F32):
        return nc.dram_tensor(name, list(shape), dt, kind="ExternalOutput").ap()

    L = depth
    xin = din("xin", [D, NTOK])
    cvec = din("cvec", [128, 16, 2])
    cache = [din("cache%d" % g, [L, WB[g], 2, 8, HD]) for g in range(3)]
    sconv = din("sconv", [L, 128, 16, 3])
    sh = din("sh", [L, 128, 16])
    biasP = din("biasP", [4, 128, 3 * 2 * 2 * 128])
    biasC = din("biasC", [4, 128, 3 * 4 * 2])
    biasN = din("biasN", [4, 4, 3 * 2 * 4])
    w_ada = din("w_ada", [L, D, 3 * D])
    w_in = din("w_in", [L, D, IN_COLS])
    w_pa = din("w_pa", [L, 1024, D])
    w_pb = din("w_pb", [L, D, D])
    w_out = din("w_out", [L, D, D])
    w_r = din("w_r", [L, 16, 128, 128])
    w_i = din("w_i", [L, 16, 128, 128])
    vecs = din("vecs", [L, 128, 48 + 16 * 9])
    fing = din("fing", [128, 16])
    identin = din("identin", [128, 128])

    yT = dout("yT", [D, NTOK])
    kvp = [dout("kvp%d" % g, [L, WB[g], 2, 8, HD]) for g in range(3)]
    convp = dout("convp", [L, 128, 16, 3])
    hpo = dout("hpo", [L, 128, 16])
    kvs = [dout("kvs%d" % g, [L, WB[g], 2, 8, HD]) for g in range(3)]
    convs = dout("convs", [L, 128, 16, 3])
    hso = dout("hso", [L, 128, 16])

    xres = nc.dram_tensor("xres", [D, NTOK], F32).ap()
    ctxK = nc.dram_tensor("ctxK", [3, 8, 128, 1024], BF16).ap()
    ctxV = nc.dram_tensor("ctxV", [3, 8, 128, 1024], BF16).ap()

    out_events = []
    st = contextlib.ExitStack()
    with st:
        S = Sched(nc, st)

        def sb(name, shape, dt):
            return st.enter_context(nc.sbuf_tensor(name, list(shape), dt))

        def pst(name, shape, dt=F32):
            return st.enter_context(nc.psum_tensor(name, list(shape), dt))

        NWB = 3
        wb = [sb("wb%d" % i, [128, 16 * 512], BF16) for i in range(NWB)]
        h = sb("h", [128, 16, TC + TS], BF16)
        att = sb("att", [128, 8, TC + TS], BF16)
        ylin = sb("ylin", [128, 16, TC + TS], BF16)
        REG = sb("regA", [128, 18432], BF16)
        ident = sb("ident", [128, 128], BF16)
        identf = sb("identf", [128, 128], F32)
        ones_f = sb("ones_f", [128, 128], F32)
        ones_b = sb("ones_b", [128, 128], BF16)
        csil = sb("csil", [128, 16, 2], BF16)
        cf = sb("cf", [128, 16, 2], F32)
        modb = [sb("mod%d" % i, [128, 48, 2], F32) for i in range(2)]
        gsb = [sb("gs%d" % i, [128, 16, 2], F32) for i in range(2)]
        vall = sb("vall", [128, L, 64], F32)
        vec = sb("vec", [128, 48 + 16 * 9], F32)
        lsc = sb("lsc", [128, 16], F32)
        lsc2 = sb("lsc2", [128, 16], F32)
        lsch = sb("lsch", [128, 32], F32)
        vech = sb("vech", [128, 32], F32)
        fg = sb("fg", [128, 16], F32)
        wrb = sb("wrb", [128, 16, 128], BF16)
        wib = sb("wib", [128, 16, 128], BF16)
        xld = [sb("xld%d" % i, [128, TC + TS], F32) for i in range(2)]
        bP = sb("bP", [128, 3, 2, 2, 128], F32)
        bC = sb("bC", [128, 3, 4, 2], F32)
        bN = sb("bN", [4, 3, 2, 4], F32)
        stage = [sb("stage%d" % i, [128, 256], F32) for i in range(3)]
        kvb = [sb("kvb%d" % i, [128, 256], BF16) for i in range(2)]
        pT = [sb("pT%d" % i, [128, 2, 256], BF16) for i in range(3)]
        halo = sb("halo", [128, 16, 3], F32)
        hcar = sb("hcar", [128, 16], F32)
        cvst = sb("cvst", [128, 16, 3], F32)
        cvss = sb("cvss", [128, 16, 3], F32)
        hsts = sb("hsts", [128, 16], F32)
        sh_t = sb("sh_t", [128, 16], F32)
        sconv_t = sb("sconv_t", [128, 16, 3], F32)
        ckt = [sb("ckt%d" % i, [128, 2, 2, 128], BF16) for i in range(4)]
        sKT = sb("sKT", [128, 4, 2, 128], BF16)
        qs = sb("qs", [128, 2, 4], BF16)
        sKn = sb("sKn", [4, 256], BF16)
        sVn = sb("sVn", [4, 256], BF16)
        sKnT = sb("sKnT", [128, 2, 4], BF16)
        sst = sb("sst", [4, 256], F32)
        spT = sb("spT", [128, 8], BF16)
        sptmp = sb("sptmp", [128, 8], F32)
        spn = sb("spn", [4, 2, 4], BF16)
        spntmp = sb("spntmp", [4, 2, 4], F32)
        sacc = sb("sacc", [128, 16], F32)
        dummy = sb("dummy_t", [128, 2], F32)
        if _STEP_HOOK is not None:
            print("SBUF remaining after alloc:", nc.sbuf_bytes_remaining)

        stmp = [xld[0][:, 0:512].rearrange("p (h c) -> p h c", h=2), xld[0][:, 512:1024].rearrange("p (h c) -> p h c", h=2),
                xld[1][:, 0:512].rearrange("p (h c) -> p h c", h=2)]
        def regv(off, n, dt):
            a = REG[:, off:off + n]
            return a.bitcast(F32) if dt == F32 else a
        sq = regv(0, 2 * (TC + TS), F32)
        rstd = regv(2100, 2 * (TC + TS), F32)
        sqb = [regv(4200, TC + TS, BF16), regv(5300, TC + TS, BF16)]
        sq2 = regv(6400, 2 * (TC + TS), F32)
        XR = [xld[0][:, :], xld[1][:, :]] + [regv(8500 + 2100 * i, 2 * (TC + TS), F32) for i in range(4)]
        XK = [("xld", 0), ("xld", 1), ("xr", 2), ("xr", 3), ("xr", 4), ("xr", 5)]
        QT = regv(0, 2 * 1024, BF16).rearrange("p (h t) -> p h t", h=2)
        KT = regv(2048, 2 * 1024, BF16).rearrange("p (h t) -> p h t", h=2)
        Vt = regv(4096, 8 * 256, BF16).rearrange("p (b c) -> p b c", b=8)
        cK = regv(6144, 2 * 1024, BF16).rearrange("p (h t) -> p h t", h=2)
        cV = regv(8192, 2 * 1024, BF16).rearrange("p (h b c) -> p h b c", h=2, b=8)
        acc = regv(10240, 4096, F32).rearrange("p (h t) -> p h t", h=2)
        den = regv(14336, 4096, F32).rearrange("p (h t) -> p h t", h=2)
        LW = TC + TS + 6
        xl = regv(0, 2 * LW, F32)
        xc = regv(2100, 2 * LW, F32)
        rt = regv(4200, 2 * LW, F32)
        it = regv(6300, 2 * LW, F32)
        a2 = regv(8400, 2 * LW, F32)
        gx = regv(10500, 2 * LW, F32)
        hst = regv(12600, 2 * LW, F32)
        zst = regv(14700, 2 * LW, F32)
        xcb = regv(16800, LW, BF16)
        merged = regv(0, 16 * (TC + TS), BF16).rearrange("p (c t) -> p c t", c=16)
        gat = regv(16448, 2 * 512, F32)
        REGKEYS = [(nm_, s_) for nm_ in ("xl", "xc", "rt", "it", "a2", "gx", "hst", "zst", "xcb") for s_ in (0, 1)] + \
                  ["QT", "KT", "Vt", "cK", "cV", "acc", "den", "xl", "xc", "rt", "it", "a2", "gx", "hst",
                   "zst", "xcb", "gat", "sq", "rstd", "sq2", ("sqb", 0), ("sqb", 1),
                   ("xld", 0), ("xld", 1), ("stmp", 0), ("stmp", 1), ("stmp", 2),
                   ("xr", 2), ("xr", 3), ("xr", 4), ("xr", 5)] + [("mg", m) for m in range(16)]

        PB = [pst("pb%d" % i, [128, 512]) for i in range(8)]
        PT7 = PB[7][:, :].bitcast(BF16)
        projrr = [0]

        def pbank():
            projrr[0] ^= 1
            return projrr[0]

        V = "vector"
        A = "scalar"
        PE = "tensor"

        def region_barrier():
            S.op(V, I("memset", dummy[:, 0:1], 0.0), writes=REGKEYS + ["dummy"])

        S.op(V, I("memset", ones_f[:], 1.0), writes=["ones_f"])
        S.op(V, I("memset", ones_b[:], 1.0), writes=["ones_b"])
        S.dma("sync", I("dma_start", out=identf[:], in_=identin), writes=["identf"])
        S.op(V, I("tensor_copy", ident[:], identf[:]), reads=["identf"], writes=["ident"])
        S.dma("sync", I("dma_start", out=cf[:], in_=cvec), writes=["cf"])
        S.dma("sync", I("dma_start", out=fg[:], in_=fing), writes=["fg"])
        for l_ in range(L):
            S.dma("sync", I("dma_start", out=vall[:, l_, :], in_=vecs[l_][:, 0:64]), writes=["vall"])
        S.op(A, I("activation", out=csil[:], in_=cf[:], func=AF.Silu), reads=["cf"], writes=["csil"])

        wstate = {"n": 0}

        def wload(pieces):
            bi = wstate["n"] % NWB
            wstate["n"] += 1
            off = 0
            views = []
            for (src, kc, ncols) in pieces:
                v = wb[bi][:, off:off + kc * ncols].rearrange("p (k n) -> p k n", k=kc)
                S.dma("gpsimd", I("dma_start",
                    out=v, in_=src.rearrange("(k p) n -> p k n", p=128)), writes=[("wb", bi)])
                views.append(v)
                off += kc * ncols
            assert off <= 16 * 512
            return bi, views

        def ntiles(nt):
            r = [(0, 512), (512, 512)]
            if nt > TC:
                r.append((TC, nt - TC))
            return r

        steps = []

        def add(pieces, fn):
            steps.append((pieces, fn))

        for l in range(L):
            xsrc = xin if l == 0 else xres

            def layer_prologue(bi, views, l=l):
                S.dma("sync", I("dma_start", out=vec[:], in_=vecs[l]), writes=["vec"])
                S.dma("gpsimd", I("dma_start", out=wrb[:], in_=w_r[l].rearrange("n c d -> c n d")),
                      writes=["wrb"])
                S.dma("gpsimd", I("dma_start", out=wib[:], in_=w_i[l].rearrange("n c d -> c n d")),
                      writes=["wib"])
                S.dma("sync", I("dma_start", out=sh_t[:], in_=sh[l]), writes=["sh_t"])
                S.dma("sync", I("dma_start", out=sconv_t[:], in_=sconv[l]), writes=["sconv_t"])
                lam_v = vec[:, 48 + 16 * 8: 48 + 16 * 9]
                S.op(A, I("activation", out=lsc[:], in_=lam_v, func=AF.Exp, scale=-1.0),
                     reads=["vec"], writes=["lsc"])
                S.op(A, I("activation", out=lsc[:], in_=lsc[:], func=AF.Ln, bias=1.0, scale=1.0),
                     reads=["lsc"], writes=["lsc"])
                S.op(V, I("tensor_scalar", lsc2[:], lsc[:], -16.0, None, ALU.mult),
                     reads=["lsc"], writes=["lsc2"])
                S.op(V, I("tensor_scalar", lsc[:], lsc[:], -8.0, None, ALU.mult),
                     reads=["lsc"], writes=["lsc"])
                S.op(V, I("tensor_scalar", lsch[:, 0:16], lsc[:], 0.5, None, ALU.mult), reads=["lsc"], writes=["lsch"])
                S.op(V, I("tensor_scalar", lsch[:, 16:32], lsc2[:], 0.5, None, ALU.mult), reads=["lsc2", "lsch"], writes=["lsch"])
                S.op(V, I("tensor_scalar", vech[:, :], vec[:, 144:176], 0.5, None, ALU.mult), reads=["vec"], writes=["vech"])
                for g in range(3):
                    n_el = (WB[g] - TS) * 2048
                    src = cache[g][l].rearrange("w a h d -> (w a h d)")[TS * 2048: WB[g] * 2048]
                    dst = kvs[g][l].rearrange("w a h d -> (w a h d)")[0: n_el]
                    ev = S.dma("sync", I("dma_start",
                        out=dst.rearrange("(p n) -> p n", p=128), in_=src.rearrange("(p n) -> p n", p=128)))
                    out_events.append(ev)
            add(None, layer_prologue)

            def make_ada(l, pc):
                mod = modb[l % 2]
                gs = gsb[l % 2]
                mk, gk = ("mod", l % 2), ("gs", l % 2)

                def ada_step(bi, views, l=l, pc=pc):
                    wv = views[0]
                    for mi in range(4):
                        m = pc * 4 + mi
                        b = pbank()
                        for k in range(16):
                            S.op(PE, I("matmul", PB[b][:, 0:2], lhsT=wv[:, k, mi * 128:(mi + 1) * 128], rhs=csil[:, k, :],
                                       start=(k == 0), stop=(k == 15)),
                                 reads=[("wb", bi), "csil"], writes=[("pb", b)], own_sync=False)
                        S.op(V, I("tensor_scalar", mod[:, m, :], PB[b][:, 0:2], vall[:, l, m:m + 1], None, ALU.add),
                             reads=[("pb", b), "vall"], writes=[mk])
                    if pc == 11:
                        for j in range(2):
                            S.op(V, I("scalar_tensor_tensor", gs[:, :, j], mod[:, 16:32, j], 1.0, vall[:, l, 48:64],
                                      ALU.add, ALU.mult),
                                 reads=[mk, "vall"], writes=[gk])
                return ([(w_ada[l][:, pc * 512:(pc + 1) * 512], 16, 512)], ada_step)

            if l == 0:
                for pc in range(12):
                    add(*make_ada(0, pc))
            mod = modb[l % 2]
            gs = gsb[l % 2]
            MK, GK = ("mod", l % 2), ("gs", l % 2)

            for ch in range(2):
                t0 = ch * TC
                nt = TC + (TS if ch == 1 else 0)
                has_s = ch == 1
                has_ctx = ch == 1
                NT = ntiles(nt)
                segs = [(0, TC, 0)] + ([(TC, TS, 1)] if has_s else [])

                def xsl(k, c0, n, xsrc=xsrc, t0=t0):
                    if c0 >= TC:
                        return xsrc[k * 128:(k + 1) * 128, T + (c0 - TC): T + (c0 - TC) + n]
                    return xsrc[k * 128:(k + 1) * 128, t0 + c0: t0 + c0 + n]

                def load_x(k, buf, nt=nt, xsl=xsl, has_s=has_s):
                    S.dma("sync", I("dma_start", out=XR[buf][:, 0:TC], in_=xsl(k, 0, TC)),
                          reads=[("xres", k)], writes=[XK[buf]])
                    if has_s:
                        S.dma("sync", I("dma_start", out=XR[buf][:, TC:nt], in_=xsl(k, TC, TS)),
                              reads=[("xres", k)], writes=[XK[buf]])

                def norm_step(bi, views, nt=nt, NT=NT, segs=segs, load_x=load_x, mod=mod, gs=gs, MK=MK, GK=GK):
                    region_barrier()
                    NR = 6
                    for i0 in range(NR - 1):
                        load_x(i0 % 16, i0 % NR)
                    for k in range(16):
                        i_ = k
                        if i_ + NR - 1 < 32:
                            load_x((i_ + NR - 1) % 16, (i_ + NR - 1) % NR)
                        xb, xk = XR[i_ % NR], XK[i_ % NR]
                        S.op(A, I("activation", out=sqb[k % 2][:, 0:nt], in_=xb[:, 0:nt], func=AF.Square),
                             reads=[xk], writes=[("sqb", k % 2)])
                        for ti, (c0, n) in enumerate(NT):
                            S.op(PE, I("matmul", PB[4 + ti][:, 0:n], lhsT=ones_b[:], rhs=sqb[k % 2][:, c0:c0 + n],
                                       start=(k == 0), stop=(k == 15)),
                                 reads=[("sqb", k % 2), "ones_b"], writes=[("pb", 4 + ti)], own_sync=False)
                    for ti, (c0, n) in enumerate(NT):
                        S.op(V, I("tensor_scalar", rstd[:, c0:c0 + n], PB[4 + ti][:, 0:n], 1.0 / D, 1e-6, ALU.mult, ALU.add),
                             reads=[("pb", 4 + ti)], writes=["rstd"])
                    S.op(A, I("activation", out=rstd[:, 0:nt], in_=rstd[:, 0:nt], func=AF.Sqrt),
                         reads=["rstd"], writes=["rstd"])
                    S.op(V, I("reciprocal", rstd[:, 0:nt], rstd[:, 0:nt]),
                         reads=["rstd"], writes=["rstd"])
                    for k in range(16):
                        i_ = 16 + k
                        if i_ + NR - 1 < 32:
                            load_x((i_ + NR - 1) % 16, (i_ + NR - 1) % NR)
                        xb, xk = XR[i_ % NR], XK[i_ % NR]
                        sqt, sqk = (sq, "sq") if k % 2 == 0 else (sq2, "sq2")
                        S.op(V, I("tensor_tensor", sqt[:, 0:nt], xb[:, 0:nt], rstd[:, 0:nt], ALU.mult),
                             reads=[xk, "rstd"], writes=[sqk])
                        for (c0, n, j) in segs:
                            S.op(A, I("activation", out=h[:, k, c0:c0 + n], in_=sqt[:, c0:c0 + n], func=AF.Identity,
                                      scale=gs[:, k, j:j + 1], bias=mod[:, k, j:j + 1]),
                                 reads=[sqk, GK, MK], writes=[("h", k)])
                add(None, norm_step)

                HK = [("h", k) for k in range(16)]

                for hp in range(4):
                    for g in range(3):
                        dil = DIL[g]

                        def perm_view(ap2, q0, dil=dil):
                            if dil == 1:
                                return ap2[:, q0 * 256:(q0 + 1) * 256]
                            if dil == 4:
                                return ap2.rearrange("p (u r) -> p r u", r=4)[:, q0, :]
                            return ap2.rearrange("p (u r) -> p r u", r=8)[:, q0 * 2:(q0 + 1) * 2, :]

                        def blk_tokens(ap3, b, dil=dil):
                            if dil == 1:
                                return ap3[:, b * 128:(b + 1) * 128]
                            if dil == 4:
                                return ap3.rearrange("p (u r) -> p r u", r=4)[:, b // 2, (b % 2) * 128:(b % 2 + 1) * 128]
                            return ap3.rearrange("p (u r) -> p r u", r=16)[:, 2 * b:2 * b + 2, :]

                        def out_rows(b, g=g, dil=dil, t0=t0):
                            lo = T - WB[g]
                            res = []
                            if dil == 1:
                                tk = t0 + b * 128
                                if tk >= lo:
                                    res.append((0, 128, tk - lo, 1))
                            elif dil == 4:
                                r, n = b // 2, b % 2
                                tk = t0 + 4 * (n * 128) + r
                                if tk >= lo:
                                    res.append((0, 128, tk - lo, 4))
                            else:
                                for rl in range(2):
                                    res.append((rl * 64, 64, t0 + 2 * b + rl, 16))
                            return res

                        def gen_blocks(g=g, dil=dil, t0=t0, has_s=has_s):
                            lo = T - WB[g]
                            res = []
                            if dil == 16:
                                for r in range(8):
                                    res.append(dict(kind="p", i=r, M=128, poff=0, tile=r, pos0=r * 128,
                                                    lhs=(lambda k, r=r: h[:, k, r:TC:8]), row0=t0 + r, rs=8))
                            else:
                                for bk in range(8):
                                    if dil == 1:
                                        tk, rs = t0 + bk * 128, 1
                                        lhs = (lambda k, bk=bk: h[:, k, bk * 128:(bk + 1) * 128])
                                    else:
                                        r, n = bk // 2, bk % 2
                                        tk, rs = t0 + 512 * n + r, 4
                                        lhs = (lambda k, r=r, n=n: h[:, k, n * 512 + r:(n + 1) * 512:4])
                                    res.append(dict(kind="p", i=bk, M=128, poff=0, tile=bk, pos0=bk * 128, lhs=lhs,
                                                    row0=(tk - lo) if tk >= lo else None, rs=rs))
                            if has_s:
                                res.append(dict(kind="s", i=99, M=TS, lhs=(lambda k: h[:, k, TC:TC + TS])))
                            return res

                        def stepA(bi, views, l=l, hp=hp, g=g, nt=nt, NT=NT, has_s=has_s, ch=ch, dil=dil,
                                  perm_view=perm_view, gen_blocks=gen_blocks):
                            wq, wk = views
                            if g == 0 and hp == 0:
                                region_barrier()
                            if ch == 1:
                                for hh in range(2):
                                    S.dma("sync", I("dma_start", out=cK[:, hh, :], in_=ctxK[g, hp * 2 + hh]),
                                          reads=[("ctxK", g, hp * 2 + hh)], writes=["cK"])
                                for s_ in range(TS):
                                    csrc = cache[g][l][s_::dil, :, hp * 2:hp * 2 + 2, :] if dil > 1 else \
                                        cache[g][l][:, :, hp * 2:hp * 2 + 2, :]
                                    S.dma("gpsimd", I("dma_start", out=ckt[s_][:], in_=csrc), writes=[("ckt", s_)])
                            if g == 0:
                                S.dma("sync", I("dma_start",
                                    out=bP[:].rearrange("p a b c d -> p (a b c d)"), in_=biasP[hp]), writes=["bP"])
                                if has_s:
                                    S.dma("sync", I("dma_start",
                                        out=bC[:].rearrange("p a b c -> p (a b c)"), in_=biasC[hp]), writes=["bC"])
                                    S.dma("sync", I("dma_start",
                                        out=bN[:].rearrange("p a b c -> p (a b c)"), in_=biasN[hp]), writes=["bN"])
                            for hh in range(2):
                                for (c0, n) in NT:
                                    b = pbank()
                                    for k in range(16):
                                        S.op(PE, I("matmul",
                                            PB[b][:, 0:n], lhsT=wq[:, k, hh * 128:(hh + 1) * 128],
                                            rhs=h[:, k, c0:c0 + n], start=(k == 0), stop=(k == 15)),
                                            reads=[("wb", bi), ("h", k)], writes=[("pb", b)], own_sync=False)
                                    if c0 >= TC:
                                        S.op(A, I("copy", qs[:, hh, :], PB[b][:, 0:n]),
                                             reads=[("pb", b)], writes=["qs"])
                                    else:
                                        if DIL[g] == 1:
                                            S.op(A, I("copy",
                                                QT[:, hh, c0:c0 + 512], PB[b][:, 0:512]),
                                                reads=[("pb", b)], writes=["QT"])
                                        else:
                                            r_ = 8 if DIL[g] == 16 else DIL[g]
                                            nu = 512 // r_
                                            u0 = c0 // r_
                                            dst = QT[:, hh, :].rearrange("p (r u) -> p r u", r=r_)[:, :, u0:u0 + nu]
                                            src = PB[b][:, 0:512].rearrange("p (u r) -> p r u", r=r_)
                                            S.op(A, I("copy", dst, src),
                                                 reads=[("pb", b)], writes=["QT"])
                            pending = [None]
                            for gb in gen_blocks():
                                b = pbank()
                                M = gb["M"]
                                for k in range(16):
                                    S.op(PE, I("matmul", PB[b][0:M, 0:256], lhsT=gb["lhs"](k), rhs=wk[:, k, :],
                                               start=(k == 0), stop=(k == 15)),
                                         reads=[("wb", bi), ("h", k)], writes=[("pb", b)], own_sync=False)
                                if gb["kind"] == "s":
                                    if pending[0] is not None:
                                        pending[0]()
                                        pending[0] = None
                                    S.op(A, I("copy", sst[:, :], PB[b][0:TS, 0:256]),
                                         reads=[("pb", b)], writes=["sst"])
                                    S.op(V, I("tensor_copy", sKn[:, :], PB[b][0:TS, 0:256]),
                                         reads=[("pb", b)], writes=["sKn"])
                                    dst = kvs[g][l][WB[g] - TS:WB[g], 0, hp * 2:hp * 2 + 2, :]
                                    ev = S.dma("sync", I("dma_start", out=dst,
                                                         in_=sst[:, :].rearrange("p (h d) -> p h d", h=2)), reads=["sst"])
                                    out_events.append(ev)
                                    for hh in range(2):
                                        S.op(PE, I("transpose", PT7[:, hh * 4:hh * 4 + 4],
                                                   sKn[:, hh * 128:(hh + 1) * 128], ident[0:TS, 0:TS]),
                                             reads=["sKn", "ident"], writes=[("pb", 7)])
                                    S.op(V, I("tensor_copy", sKnT[:].rearrange("p a b -> p (a b)"), PT7[:, 0:8]),
                                         reads=[("pb", 7)], writes=["sKnT"])
                                    continue
                                gi = gb["i"]
                                kb_i = gi % 2
                                if gb["row0"] is not None:
                                    si = gi % 3
                                    S.op(A, I("copy", stage[si][0:M, :], PB[b][0:M, 0:256]),
                                         reads=[("pb", b)], writes=[("stage", si)])
                                    r0, rs = gb["row0"], gb["rs"]
                                    dst = kvp[g][l][r0:r0 + (M - 1) * rs + 1:rs, 0, hp * 2:hp * 2 + 2, :]
                                    ev = S.dma("sync", I("dma_start", out=dst,
                                                         in_=stage[si][0:M, :].rearrange("p (h d) -> p h d", h=2)),
                                               reads=[("stage", si)])
                                    out_events.append(ev)
                                S.op(V, I("tensor_copy", kvb[kb_i][0:M, :], PB[b][0:M, 0:256]),
                                     reads=[("pb", b)], writes=[("kvb", kb_i)])
                                if pending[0] is not None:
                                    pending[0]()

                                def do_tr(M=M, kb_i=kb_i, pos0=gb["pos0"]):
                                    for hh in range(2):
                                        c_ = hh * 512 + (pos0 % 512)
                                        S.op(PE, I("transpose", PT7[:, c_:c_ + M], kvb[kb_i][0:M, hh * 128:(hh + 1) * 128],
                                                   ident[0:M, 0:M]),
                                             reads=[("kvb", kb_i), "ident"], writes=[("pb", 7)])
                                    if (pos0 + M) % 512 == 0:
                                        p0_ = pos0 + M - 512
                                        for hh in range(2):
                                            S.op(A, I("copy", KT[:, hh, p0_:p0_ + 512], PT7[:, hh * 512:(hh + 1) * 512]),
                                                 reads=[("pb", 7)], writes=["KT"])
                                pending[0] = do_tr
                            if pending[0] is not None:
                                pending[0]()
                                pending[0] = None
                            if ch == 0:
                                for hh in range(2):
                                    S.dma("sync", I("dma_start",
                                        out=ctxK[g, hp * 2 + hh], in_=KT[:, hh, :]), reads=["KT"],
                                        writes=[("ctxK", g, hp * 2 + hh)])

                        add([(w_in[l][:, QO + g * 1024 + hp * 256: QO + g * 1024 + hp * 256 + 256], 16, 256),
                             (w_in[l][:, KO + g * 1024 + hp * 256: KO + g * 1024 + hp * 256 + 256], 16, 256)], stepA)

                        def stepB(bi, views, l=l, hp=hp, g=g, nt=nt, NT=NT, has_s=has_s, has_ctx=has_ctx, ch=ch,
                                  perm_view=perm_view, gen_blocks=gen_blocks, dil=dil):
                            wv = views[0]
                            if ch == 1:
                                for hh in range(2):
                                    S.dma("sync", I("dma_start", out=cV[:, hh, :, :],
                                                    in_=ctxV[g, hp * 2 + hh].rearrange("p (b c) -> p b c", b=8)),
                                          reads=[("ctxV", g, hp * 2 + hh)], writes=["cV"])
                            units = []
                            if dil == 1:
                                if has_ctx:
                                    units.append(("c", 7, [(0, 1)]))
                                for kb in range(8):
                                    units.append(("k", kb, [(kb, 0)] + ([(kb + 1, 1)] if kb < 7 else [])))
                            elif dil == 4:
                                for r in range(4):
                                    if has_ctx:
                                        units.append(("c", 2 * r + 1, [(2 * r, 1)]))
                                    units.append(("k", 2 * r, [(2 * r, 0), (2 * r + 1, 1)]))
                                    units.append(("k", 2 * r + 1, [(2 * r + 1, 0)]))
                            else:
                                for m in range(8):
                                    if has_ctx:
                                        units.append(("c", m, [(m, 1)]))
                                    units.append(("k", m, [(m, 0)]))
                            last_unit_of_q = {}
                            for ui, (src, kb, qs_) in enumerate(units):
                                for (qb, ty) in qs_:
                                    last_unit_of_q[qb] = ui
                            SRING = [2, 3]
                            AHEAD = 2

                            def emit_S(ui):
                                src, kb, qs_ = units[ui]
                                nq = len(qs_)
                                q0 = qs_[0][0]
                                sbk = SRING[ui % 2]
                                ti = ui % 3
                                for hh in range(2):
                                    ktile = (KT if src == "k" else cK)[:, hh, kb * 128:(kb + 1) * 128]
                                    S.op(PE, I("matmul", PB[sbk][:, hh * 256:hh * 256 + nq * 128], lhsT=ktile,
                                               rhs=QT[:, hh, q0 * 128:(q0 + nq) * 128], start=True, stop=True,
                                               skip_group_check=True),
                                         reads=["KT" if src == "k" else "cK", "QT"], writes=[("pb", sbk)], own_sync=False)
                                ty0 = qs_[0][1]
                                bias_ap = bP[:, g, :, ty0:ty0 + nq, :].rearrange("p h a b -> p h (a b)")
                                psv = PB[sbk][:, :].rearrange("p (h c) -> p h c", h=2)[:, :, 0:nq * 128]
                                S.op(V, I("scalar_tensor_tensor", stmp[ti][:, :, 0:nq * 128], psv, SCALE, bias_ap,
                                          ALU.mult, ALU.add),
                                     reads=[("pb", sbk), "bP"], writes=[("stmp", ti)])
                                S.op(A, I("activation", out=pT[ti][:, :, 0:nq * 128], in_=stmp[ti][:, :, 0:nq * 128],
                                          func=AF.Exp),
                                     reads=[("stmp", ti)], writes=[("pT", ti)])

                            for ui in range(min(AHEAD, len(units))):
                                emit_S(ui)
                            for gb in gen_blocks():
                                b = pbank()
                                M = gb["M"]
                                for k in range(16):
                                    S.op(PE, I("matmul", PB[b][0:M, 0:256], lhsT=gb["lhs"](k), rhs=wv[:, k, 0:256],
                                               start=(k == 0), stop=(k == 15)),
                                         reads=[("wb", bi), ("h", k)], writes=[("pb", b)], own_sync=False)
                                if gb["kind"] == "s":
                                    S.op(A, I("copy", sst[:, :], PB[b][0:TS, 0:256]),
                                         reads=[("pb", b)], writes=["sst"])
                                    S.op(V, I("tensor_copy", sVn[:, :], PB[b][0:TS, 0:256]),
                                         reads=[("pb", b)], writes=["sVn"])
                                    dst = kvs[g][l][WB[g] - TS:WB[g], 1, hp * 2:hp * 2 + 2, :]
                                    ev = S.dma("sync", I("dma_start", out=dst,
                                                         in_=sst[:, :].rearrange("p (h d) -> p h d", h=2)), reads=["sst"])
                                    out_events.append(ev)
                                    continue
                                gi = gb["i"]
                                if gb["row0"] is not None:
                                    si = gi % 3
                                    S.op(A, I("copy", stage[si][0:M, :], PB[b][0:M, 0:256]),
                                         reads=[("pb", b)], writes=[("stage", si)])
                                    r0, rs = gb["row0"], gb["rs"]
                                    dst = kvp[g][l][r0:r0 + (M - 1) * rs + 1:rs, 1, hp * 2:hp * 2 + 2, :]
                                    ev = S.dma("sync", I("dma_start", out=dst,
                                                         in_=stage[si][0:M, :].rearrange("p (h d) -> p h d", h=2)),
                                               reads=[("stage", si)])
                                    out_events.append(ev)
                                tile_, poff = gb["tile"], gb["poff"]
                                if poff == 0:
                                    S.op(V, I("tensor_copy", Vt[0:M, tile_, :], PB[b][0:M, 0:256]),
                                         reads=[("pb", b)], writes=["Vt"])
                                else:
                                    kb_i = gi % 2
                                    S.op(V, I("tensor_copy", kvb[kb_i][0:M, :], PB[b][0:M, 0:256]),
                                         reads=[("pb", b)], writes=[("kvb", kb_i)])
                                    S.dma("sync", I("dma_start", out=Vt[poff:poff + M, tile_, :], in_=kvb[kb_i][0:M, :]),
                                          reads=[("kvb", kb_i)], writes=["Vt"])
                            if ch == 0:
                                for hh in range(2):
                                    S.dma("sync", I("dma_start",
                                        out=ctxV[g, hp * 2 + hh].rearrange("p (b c) -> p b c", b=8),
                                        in_=Vt[:, :, hh * 128:(hh + 1) * 128]), reads=["Vt"],
                                        writes=[("ctxV", g, hp * 2 + hh)])
                            fresh = {}
                            done_q = set()
                            for ui, (src, kb, qs_) in enumerate(units):
                                if ui + AHEAD < len(units):
                                    emit_S(ui + AHEAD)
                                nq = len(qs_)
                                ti = ui % 3
                                for hh in range(2):
                                    vtile = (Vt[:, kb, hh * 128:(hh + 1) * 128] if src == "k" else cV[:, hh, kb, :])
                                    for qi, (qb, ty) in enumerate(qs_):
                                        qt = qb // 2
                                        ob = 4 + 2 * (qt % 2)
                                        first = fresh.get(qt, True)
                                        fresh[qt] = False
                                        oc = hh * 256 + (qb % 2) * 128
                                        S.op(PE, I("matmul", PB[ob][:, oc:oc + 128], lhsT=vtile,
                                                   rhs=pT[ti][:, hh, qi * 128:(qi + 1) * 128],
                                                   start=first, stop=False, skip_group_check=True),
                                             reads=["Vt" if src == "k" else "cV", ("pT", ti)], writes=[("pb", ob)],
                                             own_sync=False)
                                        S.op(PE, I("matmul", PB[ob + 1][:, oc:oc + 128], lhsT=ones_b[:],
                                                   rhs=pT[ti][:, hh, qi * 128:(qi + 1) * 128],
                                                   start=first, stop=False, skip_group_check=True),
                                             reads=["ones_b", ("pT", ti)], writes=[("pb", ob + 1)], own_sync=False)
                                for qt in sorted(fresh.keys()):
                                    if qt in done_q:
                                        continue
                                    if last_unit_of_q[2 * qt] > ui or last_unit_of_q[2 * qt + 1] > ui:
                                        continue
                                    done_q.add(qt)
                                    ob = 4 + 2 * (qt % 2)
                                    if dil == 16:
                                        pairs = []
                                        for hh in range(2):
                                            pairs.append((perm_view(acc[:, hh, :], qt), perm_view(den[:, hh, :], qt),
                                                          PB[ob][:, hh * 256:(hh + 1) * 256].rearrange("p (r u) -> p r u", r=2),
                                                          PB[ob + 1][:, hh * 256:(hh + 1) * 256].rearrange("p (r u) -> p r u", r=2)))
                                    else:
                                        if dil == 1:
                                            av = acc[:, :, qt * 256:(qt + 1) * 256]
                                            dv = den[:, :, qt * 256:(qt + 1) * 256]
                                        else:
                                            av = acc[:, :, :].rearrange("p h (u r) -> p h r u", r=4)[:, :, qt, :]
                                            dv = den[:, :, :].rearrange("p h (u r) -> p h r u", r=4)[:, :, qt, :]
                                        pairs = [(av, dv, PB[ob][:, :].rearrange("p (h c) -> p h c", h=2),
                                                  PB[ob + 1][:, :].rearrange("p (h c) -> p h c", h=2))]
                                    for (av, dv, osrc, dsrc) in pairs:
                                        if g == 0:
                                            S.op(A, I("copy", av, osrc), reads=[("pb", ob)], writes=["acc"])
                                            S.op(V, I("tensor_copy", dv, dsrc), reads=[("pb", ob + 1)], writes=["den"])
                                        else:
                                            S.op(V, I("tensor_tensor", av, av, osrc, ALU.add),
                                                 reads=[("pb", ob), "acc"], writes=["acc"])
                                            S.op(V, I("tensor_tensor", dv, dv, dsrc, ALU.add),
                                                 reads=[("pb", ob + 1), "den"], writes=["den"])
                            if has_s:
                                for s in range(TS):
                                    for hh in range(2):
                                        c_ = (s * 2 + hh) * 128
                                        S.op(PE, I("transpose", PT7[:, c_:c_ + 128], ckt[s][:, 0, hh, :], ident[:]),
                                             reads=[("ckt", s), "ident"], writes=[("pb", 7)])
                                S.op(A, I("copy", sKT[:].rearrange("p s a b -> p (s a b)"), PT7[:, 0:1024]),
                                     reads=[("pb", 7)], writes=["sKT"])
                                sbk = 3
                                for s in range(TS):
                                    for hh in range(2):
                                        S.op(PE, I("matmul", PB[sbk][:, s * 2 + hh:s * 2 + hh + 1], lhsT=sKT[:, s, hh, :],
                                                   rhs=qs[:, hh, s:s + 1], start=True, stop=True, skip_group_check=True),
                                             reads=["sKT", "qs"], writes=[("pb", sbk)], own_sync=False)
                                S.op(V, I("scalar_tensor_tensor", sptmp[:, :], PB[sbk][:, 0:8], SCALE,
                                          bC[:, g, :, :].rearrange("p a b -> p (a b)"), ALU.mult, ALU.add),
                                     reads=[("pb", sbk), "bC"], writes=["sptmp"])
                                S.op(A, I("activation", out=spT[:, :], in_=sptmp[:, :], func=AF.Exp),
                                     reads=["sptmp"], writes=["spT"])
                                for s in range(TS):
                                    for hh in range(2):
                                        first = (s == 0 and hh == 0)
                                        S.op(PE, I("matmul", PB[6][:, hh * 4 + s:hh * 4 + s + 1], lhsT=ckt[s][:, 1, hh, :],
                                                   rhs=spT[:, s * 2 + hh:s * 2 + hh + 1], start=first, stop=False,
                                                   skip_group_check=True),
                                             reads=[("ckt", s), "spT"], writes=[("pb", 6)], own_sync=False)
                                        S.op(PE, I("matmul", PB[6][:, 8 + hh * 4 + s:8 + hh * 4 + s + 1], lhsT=ones_b[:],
                                                   rhs=spT[:, s * 2 + hh:s * 2 + hh + 1], start=False, stop=False,
                                                   skip_group_check=True),
                                             reads=["ones_b", "spT"], writes=[("pb", 6)], own_sync=False)
                                sbk = 2
                                for hh in range(2):
                                    S.op(PE, I("matmul",
                                        PB[sbk][0:TS, hh * 4:hh * 4 + 4], lhsT=sKnT[:, hh, :], rhs=qs[:, hh, :],
                                        start=True, stop=True, skip_group_check=True),
                                        reads=["sKnT", "qs"], writes=[("pb", sbk)], own_sync=False)
                                S.op(V, I("scalar_tensor_tensor",
                                    spntmp[:].rearrange("p a b -> p (a b)"), PB[sbk][0:TS, 0:8], SCALE,
                                    bN[:, g, :, :].rearrange("p a b -> p (a b)"), ALU.mult, ALU.add),
                                    reads=[("pb", sbk), "bN"], writes=["spntmp"])
                                S.op(A, I("activation", out=spn[:].rearrange("p a b -> p (a b)"),
                                                               in_=spntmp[:].rearrange("p a b -> p (a b)"), func=AF.Exp),
                                     reads=["spntmp"], writes=["spn"])
                                for hh in range(2):
                                    S.op(PE, I("matmul",
                                        PB[6][:, hh * 4:hh * 4 + 4], lhsT=sVn[:, hh * 128:(hh + 1) * 128],
                                        rhs=spn[:, hh, :], start=False, stop=False, skip_group_check=True),
                                        reads=["sVn", "spn"], writes=[("pb", 6)], own_sync=False)
                                    S.op(PE, I("matmul",
                                        PB[6][:, 8 + hh * 4:8 + hh * 4 + 4], lhsT=ones_b[0:TS, :],
                                        rhs=spn[:, hh, :], start=False, stop=False, skip_group_check=True),
                                        reads=["ones_b", "spn"], writes=[("pb", 6)], own_sync=False)
                            if has_s:
                                if g == 0:
                                    S.op(V, I("tensor_copy", sacc[:, :], PB[6][:, 0:16]),
                                         reads=[("pb", 6)], writes=["sacc"])
                                else:
                                    S.op(V, I("tensor_tensor", sacc[:, :], sacc[:, :], PB[6][:, 0:16], ALU.add),
                                         reads=[("pb", 6), "sacc"], writes=["sacc"])
                            if g == 2:
                                wz = views[1]
                                if has_s:
                                    S.op(V, I("reciprocal", sacc[:, 8:16], sacc[:, 8:16]),
                                         reads=["sacc"], writes=["sacc"])
                                    S.op(V, I("tensor_tensor", sacc[:, 0:8], sacc[:, 0:8], sacc[:, 8:16], ALU.mult),
                                         reads=["sacc"], writes=["sacc"])
                                for hh in range(2):
                                    S.op(V, I("reciprocal", den[:, hh, :], den[:, hh, :]),
                                         reads=["den"], writes=["den"])
                                    S.op(V, I("tensor_tensor", acc[:, hh, :], acc[:, hh, :], den[:, hh, :],
                                                                             ALU.mult), reads=["den", "acc"], writes=["acc"])
                                    for (c0, n) in NT:
                                        b = pbank()
                                        for k in range(16):
                                            S.op(PE, I("matmul",
                                                PB[b][:, 0:n], lhsT=wz[:, k, hh * 128:(hh + 1) * 128],
                                                rhs=h[:, k, c0:c0 + n], start=(k == 0), stop=(k == 15)),
                                                reads=[("wb", bi), ("h", k)], writes=[("pb", b)], own_sync=False)
                                        ti = b
                                        S.op(A, I("activation",
                                            out=stmp[ti][:, 0, 0:n] if n <= 256 else den[:, hh, c0:c0 + n],
                                            in_=PB[b][:, 0:n], func=AF.Silu),
                                            reads=[("pb", b)], writes=[("stmp", ti), "den"])
                                        if c0 >= TC:
                                            S.op(V, I("tensor_tensor",
                                                att[:, hp * 2 + hh, c0:c0 + n], sacc[:, hh * 4:hh * 4 + 4],
                                                stmp[ti][:, 0, 0:n], ALU.mult),
                                                reads=["sacc", ("stmp", ti)], writes=[("att", hp * 2 + hh)])
                                        else:
                                            S.op(V, I("tensor_tensor",
                                                att[:, hp * 2 + hh, c0:c0 + n], acc[:, hh, c0:c0 + n],
                                                den[:, hh, c0:c0 + n], ALU.mult),
                                                reads=["acc", "den"], writes=[("att", hp * 2 + hh)])

                        pcs = [(w_in[l][:, VO + g * 1024 + hp * 256: VO + g * 1024 + hp * 256 + 256], 16, 256)]
                        if g == 2:
                            pcs.append((w_in[l][:, ZAO + hp * 256: ZAO + hp * 256 + 256], 16, 256))
                        add(pcs, stepB)

                for np_ in range(8):
                    def lru_step(bi, views, l=l, np_=np_, nt=nt, NT=NT, has_s=has_s, ch=ch, segs=segs):
                        wx, wz = views
                        if np_ == 0:
                            region_barrier()
                        for ni in range(2):
                            n_ = np_ * 2 + ni
                            W_ = nt + 3 * len(segs) - 3
                            SEG = [(0, 512), (512, W_)]
                            if ch == 0:
                                S.op(V, I("memset", xl[:, 0:3], 0.0), writes=[("xl", 0)])
                            else:
                                S.op(V, I("tensor_copy", xl[:, 0:3], halo[:, n_, :]), reads=["halo"], writes=[("xl", 0)])
                                S.op(V, I("tensor_copy", xl[:, TC + 3:TC + 6], sconv_t[:, n_, :]),
                                     reads=["sconv_t"], writes=[("xl", 1)])
                            for ti_, (c0, n) in enumerate(NT):
                                b = pbank()
                                for k in range(16):
                                    S.op(PE, I("matmul", PB[b][:, 0:n], lhsT=wx[:, k, ni * 128:(ni + 1) * 128],
                                               rhs=h[:, k, c0:c0 + n], start=(k == 0), stop=(k == 15)),
                                         reads=[("wb", bi), ("h", k)], writes=[("pb", b)], own_sync=False)
                                xo = c0 + 3 if c0 < TC else TC + 6
                                S.op(A, I("copy", xl[:, xo:xo + n], PB[b][:, 0:n]),
                                     reads=[("pb", b)], writes=[("xl", min(ti_, 1))])
                            for ti_, (c0, n) in enumerate(NT):
                                b = pbank()
                                for k in range(16):
                                    S.op(PE, I("matmul", PB[b][:, 0:n], lhsT=wz[:, k, ni * 128:(ni + 1) * 128],
                                               rhs=h[:, k, c0:c0 + n], start=(k == 0), stop=(k == 15)),
                                         reads=[("wb", bi), ("h", k)], writes=[("pb", b)], own_sync=False)
                                S.op(A, I("activation", out=zst[:, c0:c0 + n], in_=PB[b][:, 0:n], func=AF.Tanh, scale=0.5),
                                     reads=[("pb", b)], writes=[("zst", min(ti_, 1))])
                                S.op(V, I("scalar_tensor_tensor", zst[:, c0:c0 + n], zst[:, c0:c0 + n], 1.0, PB[b][:, 0:n],
                                          ALU.add, ALU.mult),
                                     reads=[("pb", b), ("zst", min(ti_, 1))], writes=[("zst", min(ti_, 1))])
                            cw = lambda tap, n_=n_: vec[:, 64 + tap * 16 + n_: 64 + tap * 16 + n_ + 1]
                            cb = vec[:, 128 + n_:128 + n_ + 1]
                            gt = [[(0, 512)], [(512, 512)] + ([(TC + 3, TS)] if has_s else [])]
                            for sg, (ca, cz) in enumerate(SEG):
                                xlk = [("xl", 0)] if sg == 0 else [("xl", 0), ("xl", 1)]
                                S.op(V, I("tensor_scalar", xc[:, ca:cz], xl[:, ca:cz], cw(0), cb, ALU.mult, ALU.add),
                                     reads=xlk + ["vec"], writes=[("xc", sg)])
                                for tap in range(1, 4):
                                    S.op(V, I("scalar_tensor_tensor", xc[:, ca:cz], xl[:, ca + tap:cz + tap], cw(tap),
                                              xc[:, ca:cz], ALU.mult, ALU.add),
                                         reads=xlk + ["vec", ("xc", sg)], writes=[("xc", sg)])
                                if sg == 1:
                                    if ch == 1:
                                        S.op(V, I("tensor_copy", cvst[:, n_, :], xl[:, TC:TC + 3]),
                                             reads=[("xl", 1)], writes=["cvst"])
                                        S.op(V, I("tensor_copy", cvss[:, n_, :], xl[:, TC + 7:TC + 10]),
                                             reads=[("xl", 1)], writes=["cvss"])
                                    else:
                                        S.op(V, I("tensor_copy", halo[:, n_, :], xl[:, TC:TC + 3]),
                                             reads=[("xl", 1)], writes=["halo"])
                                S.op(V, I("tensor_copy", xcb[:, ca:cz], xc[:, ca:cz]), reads=[("xc", sg)], writes=[("xcb", sg)])
                                for (c0, n) in gt[sg]:
                                    for (wmat, dst, bcol, nm) in ((wrb, rt, 144, "rt"), (wib, it, 160, "it")):
                                        b = pbank()
                                        S.op(PE, I("matmul", PB[b][:, 0:n], lhsT=wmat[:, n_, :], rhs=xcb[:, c0:c0 + n],
                                                   start=True, stop=True),
                                             reads=["wrb", "wib", ("xcb", sg)], writes=[("pb", b)], own_sync=False)
                                        S.op(A, I("activation", out=dst[:, c0:c0 + n], in_=PB[b][:, 0:n], func=AF.Tanh,
                                                  scale=0.5, bias=vech[:, bcol - 144 + n_:bcol - 144 + n_ + 1]),
                                             reads=[("pb", b), "vech"], writes=[(nm, sg)])
                            for sg, (ca, cz) in enumerate(SEG):
                                S.op(V, I("scalar_tensor_tensor", gx[:, ca:cz], it[:, ca:cz], 1.0, xc[:, ca:cz], ALU.add, ALU.mult),
                                     reads=[("it", sg), ("xc", sg)], writes=[("gx", sg)])
                            for sg, (ca, cz) in enumerate(SEG):
                                S.op(A, I("activation", out=a2[:, ca:cz], in_=rt[:, ca:cz], func=AF.Exp,
                                          scale=lsch[:, 16 + n_:16 + n_ + 1], bias=lsch[:, 16 + n_:16 + n_ + 1]),
                                     reads=[("rt", sg), "lsch"], writes=[("a2", sg)])
                                S.op(A, I("activation", out=rt[:, ca:cz], in_=rt[:, ca:cz], func=AF.Exp,
                                          scale=lsch[:, n_:n_ + 1], bias=lsch[:, n_:n_ + 1]),
                                     reads=[("rt", sg), "lsch"], writes=[("rt", sg)])
                            for sg, (ca, cz) in enumerate(SEG):
                                S.op(A, I("activation", out=a2[:, ca:cz], in_=a2[:, ca:cz], func=AF.Relu, scale=-1.0, bias=1.0),
                                     reads=[("a2", sg)], writes=[("a2", sg)])
                            for sg, (ca, cz) in enumerate(SEG):
                                S.op(A, I("activation", out=a2[:, ca:cz], in_=a2[:, ca:cz], func=AF.Sqrt),
                                     reads=[("a2", sg)], writes=[("a2", sg)])
                            for sg, (ca, cz) in enumerate(SEG):
                                S.op(V, I("scalar_tensor_tensor", gx[:, ca:cz], gx[:, ca:cz], 0.5, a2[:, ca:cz], ALU.mult, ALU.mult),
                                     reads=[("gx", sg), ("a2", sg)], writes=[("gx", sg)])
                                if sg == 0:
                                    init = 0.0 if ch == 0 else hcar[:, n_:n_ + 1]
                                    S.op(V, I("tensor_tensor_scan", hst[:, 0:512], rt[:, 0:512], gx[:, 0:512], init,
                                              ALU.mult, ALU.add),
                                         reads=[("rt", 0), ("gx", 0), "hcar"], writes=[("hst", 0)])
                                else:
                                    S.op(V, I("tensor_tensor_scan", hst[:, 512:TC], rt[:, 512:TC], gx[:, 512:TC],
                                              hst[:, 511:512], ALU.mult, ALU.add),
                                         reads=[("rt", 1), ("gx", 1), ("hst", 0)], writes=[("hst", 1)])
                                    if has_s:
                                        S.op(V, I("tensor_tensor_scan", hst[:, TC + 3:TC + 3 + TS], rt[:, TC + 3:TC + 3 + TS],
                                                  gx[:, TC + 3:TC + 3 + TS], sh_t[:, n_:n_ + 1], ALU.mult, ALU.add),
                                             reads=[("rt", 1), ("gx", 1), "sh_t"], writes=[("hst", 1)])
                            S.op(V, I("tensor_copy", hcar[:, n_:n_ + 1], hst[:, TC - 1:TC]),
                                 reads=[("hst", 1)], writes=["hcar"])
                            if ch == 1:
                                S.op(V, I("tensor_copy", hsts[:, n_:n_ + 1], hst[:, TC + 6:TC + 7]),
                                     reads=[("hst", 1)], writes=["hsts"])
                            for ti_, (c0, n) in enumerate(NT):
                                ho = c0 if c0 < TC else TC + 3
                                sg = min(ti_, 1)
                                S.op(V, I("scalar_tensor_tensor", ylin[:, n_, c0:c0 + n], zst[:, c0:c0 + n], 0.5, hst[:, ho:ho + n],
                                          ALU.mult, ALU.mult),
                                     reads=[("hst", sg), ("zst", sg)], writes=[("ylin", n_)])
                        if np_ == 7 and ch == 1:
                            for (src_t, dst_d, key) in ((cvst, convp[l], "cvst"), (cvss, convs[l], "cvss")):
                                ev = S.dma("sync", I("dma_start", out=dst_d, in_=src_t[:]),
                                           reads=[key])
                                out_events.append(ev)
                            for (src_t, dst_d, key) in ((hcar, hpo[l], "hcar"), (hsts, hso[l], "hsts")):
                                ev = S.dma("sync", I("dma_start", out=dst_d, in_=src_t[:]),
                                           reads=[key])
                                out_events.append(ev)
                    add([(w_in[l][:, XLO + np_ * 256: XLO + np_ * 256 + 256], 16, 256),
                         (w_in[l][:, ZLO + np_ * 256: ZLO + np_ * 256 + 256], 16, 256)], lru_step)
                    if ch == 1 and l + 1 < L:
                        add(*make_ada(l + 1, np_))

                for mp in range(8):
                    def merge1(bi, views, mp=mp, NT=NT):
                        wga, wpa = views
                        if mp == 0:
                            region_barrier()
                        for mi in range(2):
                            m = mp * 2 + mi
                            for (c0, n) in NT:
                                b = pbank()
                                for k in range(16):
                                    S.op(PE, I("matmul",
                                        PB[b][:, 0:n], lhsT=wga[:, k, mi * 128:(mi + 1) * 128],
                                        rhs=h[:, k, c0:c0 + n], start=(k == 0), stop=(k == 15)),
                                        reads=[("wb", bi), ("h", k)], writes=[("pb", b)], own_sync=False)
                                S.op(A, I("activation", out=gat[:, 0:n], in_=PB[b][:, 0:n], func=AF.Sigmoid),
                                     reads=[("pb", b)], writes=["gat"])
                                b2 = 4 + (b % 2)
                                for k in range(8):
                                    S.op(PE, I("matmul",
                                        PB[b2][:, 0:n], lhsT=wpa[:, k, mi * 128:(mi + 1) * 128],
                                        rhs=att[:, k, c0:c0 + n], start=(k == 0), stop=(k == 7)),
                                        reads=[("wb", bi), ("att", k)], writes=[("pb", b2)], own_sync=False)
                                S.op(V, I("tensor_tensor",
                                    merged[:, m, c0:c0 + n], gat[:, 0:n], PB[b2][:, 0:n], ALU.mult),
                                    reads=["gat", ("pb", b2)], writes=[("mg", m)])
                    add([(w_in[l][:, GAO + mp * 256: GAO + mp * 256 + 256], 16, 256),
                         (w_pa[l][:, mp * 256: mp * 256 + 256], 8, 256)], merge1)

                    def merge2(bi, views, mp=mp, NT=NT):
                        wgl, wpb = views
                        for mi in range(2):
                            m = mp * 2 + mi
                            for (c0, n) in NT:
                                b = pbank()
                                for k in range(16):
                                    S.op(PE, I("matmul",
                                        PB[b][:, 0:n], lhsT=wgl[:, k, mi * 128:(mi + 1) * 128],
                                        rhs=h[:, k, c0:c0 + n], start=(k == 0), stop=(k == 15)),
                                        reads=[("wb", bi), ("h", k)], writes=[("pb", b)], own_sync=False)
                                S.op(A, I("activation", out=gat[:, 0:n], in_=PB[b][:, 0:n], func=AF.Sigmoid),
                                     reads=[("pb", b)], writes=["gat"])
                                b2 = 4 + (b % 2)
                                for k in range(16):
                                    S.op(PE, I("matmul",
                                        PB[b2][:, 0:n], lhsT=wpb[:, k, mi * 128:(mi + 1) * 128],
                                        rhs=ylin[:, k, c0:c0 + n], start=(k == 0), stop=(k == 15)),
                                        reads=[("wb", bi), ("ylin", k)], writes=[("pb", b2)], own_sync=False)
                                S.op(V, I("tensor_tensor", gat[:, 0:n], gat[:, 0:n], PB[b2][:, 0:n], ALU.mult),
                                     reads=["gat", ("pb", b2)], writes=["gat"])
                                S.op(V, I("tensor_tensor",
                                    merged[:, m, c0:c0 + n], merged[:, m, c0:c0 + n], gat[:, 0:n], ALU.add),
                                    reads=["gat", ("mg", m)], writes=[("mg", m)])
                    add([(w_in[l][:, GLO + mp * 256: GLO + mp * 256 + 256], 16, 256),
                         (w_pb[l][:, mp * 256: mp * 256 + 256], 16, 256)], merge2)
                    if ch == 1 and l + 1 < L and mp < 4:
                        add(*make_ada(l + 1, 8 + mp))

                for op_ in range(4):
                    def out_step(bi, views, l=l, op_=op_, nt=nt, NT=NT, segs=segs, load_x=load_x, t0=t0, has_s=has_s, mod=mod, MK=MK):
                        wo = views[0]
                        if op_ == 0:
                            region_barrier()
                        load_x(op_ * 4, (op_ * 4) % 2)
                        for mi in range(4):
                            m = op_ * 4 + mi
                            buf = m % 2
                            if mi + 1 < 4:
                                load_x(m + 1, (m + 1) % 2)
                            for (c0, n) in NT:
                                b = pbank()
                                for k in range(16):
                                    S.op(PE, I("matmul",
                                        PB[b][:, 0:n], lhsT=wo[:, k, mi * 128:(mi + 1) * 128],
                                        rhs=merged[:, k, c0:c0 + n], start=(k == 0), stop=(k == 15)),
                                        reads=[("wb", bi), ("mg", k)], writes=[("pb", b)], own_sync=False)
                                j = 0 if c0 < TC else 1
                                S.op(V, I("scalar_tensor_tensor",
                                    xld[buf][:, c0:c0 + n], PB[b][:, 0:n], mod[:, 32 + m, j:j + 1], xld[buf][:, c0:c0 + n],
                                    ALU.mult, ALU.add),
                                    reads=[("pb", b), MK, ("xld", buf)], writes=[("xld", buf)])
                            S.dma("sync", I("dma_start",
                                out=xres[m * 128:(m + 1) * 128, t0:t0 + TC], in_=xld[buf][:, 0:TC]),
                                reads=[("xld", buf)], writes=[("xres", m)])
                            if has_s:
                                S.dma("sync", I("dma_start",
                                    out=xres[m * 128:(m + 1) * 128, T:T + TS], in_=xld[buf][:, TC:TC + TS]),
                                    reads=[("xld", buf)], writes=[("xres", m)])
                    add([(w_out[l][:, op_ * 512:(op_ + 1) * 512], 16, 512)], out_step)

        def final_step(bi, views):
            region_barrier()
            for ch in range(2):
                t0 = ch * TC
                nt = TC + (TS if ch == 1 else 0)
                NT = ntiles(nt)

                def ld(k, buf):
                    S.dma("sync", I("dma_start", out=XR[buf][:, 0:TC], in_=xres[k * 128:(k + 1) * 128, t0:t0 + TC]),
                          reads=[("xres", k)], writes=[XK[buf]])
                    if ch == 1:
                        S.dma("sync", I("dma_start", out=XR[buf][:, TC:nt], in_=xres[k * 128:(k + 1) * 128, T:T + TS]),
                              reads=[("xres", k)], writes=[XK[buf]])
                NR = 6
                base = ch * 32
                for i0 in range(NR - 1):
                    ld(i0 % 16, (base + i0) % NR)
                for k in range(16):
                    i_ = k
                    if i_ + NR - 1 < 32:
                        ld((i_ + NR - 1) % 16, (base + i_ + NR - 1) % NR)
                    xb, xk = XR[(base + i_) % NR], XK[(base + i_) % NR]
                    S.op(A, I("activation", out=sqb[k % 2][:, 0:nt], in_=xb[:, 0:nt], func=AF.Square),
                         reads=[xk], writes=[("sqb", k % 2)])
                    for ti, (c0, n) in enumerate(NT):
                        S.op(PE, I("matmul", PB[4 + ti][:, 0:n], lhsT=ones_b[:], rhs=sqb[k % 2][:, c0:c0 + n],
                                   start=(k == 0), stop=(k == 15)),
                             reads=[("sqb", k % 2), "ones_b"], writes=[("pb", 4 + ti)], own_sync=False)
                for ti, (c0, n) in enumerate(NT):
                    S.op(V, I("tensor_scalar", rstd[:, c0:c0 + n], PB[4 + ti][:, 0:n], 1.0 / D, 1e-6, ALU.mult, ALU.add),
                         reads=[("pb", 4 + ti)], writes=["rstd"])
                S.op(A, I("activation", out=rstd[:, 0:nt], in_=rstd[:, 0:nt], func=AF.Sqrt),
                     reads=["rstd"], writes=["rstd"])
                S.op(V, I("reciprocal", rstd[:, 0:nt], rstd[:, 0:nt]),
                     reads=["rstd"], writes=["rstd"])
                for k in range(16):
                    i_ = 16 + k
                    if i_ + NR - 1 < 32:
                        ld((i_ + NR - 1) % 16, (base + i_ + NR - 1) % NR)
                    xb, xk = XR[(base + i_) % NR], XK[(base + i_) % NR]
                    S.op(V, I("scalar_tensor_tensor", xb[:, 0:nt], xb[:, 0:nt], fg[:, k:k + 1], rstd[:, 0:nt],
                              ALU.mult, ALU.mult),
                         reads=[xk, "rstd", "fg"], writes=[xk])
                    ev = S.dma("sync", I("dma_start", out=yT[k * 128:(k + 1) * 128, t0:t0 + TC], in_=xb[:, 0:TC]),
                               reads=[xk])
                    out_events.append(ev)
                    if ch == 1:
                        ev = S.dma("sync", I("dma_start", out=yT[k * 128:(k + 1) * 128, T:T + TS], in_=xb[:, TC:nt]),
                                   reads=[xk])
                        out_events.append(ev)
        add(None, final_step)

        wsteps = [i for i, (p, f) in enumerate(steps) if p is not None]
        loaded = {}
        nxt = 0
        import os as _os
        _maxs = int(_os.environ.get('MAXSTEPS', '100000'))
        steps = steps[:_maxs]
        wsteps = [j for j in wsteps if j < _maxs]
        for i, (p, f) in enumerate(steps):
            upcoming = [j for j in wsteps[nxt:nxt + NWB]]
            ahead = [j for j in wsteps if j >= i][:NWB]
            for j in ahead:
                if j not in loaded and j >= (wsteps[nxt] if nxt < len(wsteps) else 1 << 30):
                    loaded[j] = wload(steps[j][0])
                    nxt += 1
            if _STEP_HOOK is not None:
                _STEP_HOOK(i, f.__name__)
            if p is None:
                f(None, None)
            else:
                bi, views = loaded.pop(i)
                f(bi, views)
        S.finish(out_events)
    return nc


_CACHE = {}


def _fm(v):
    v = np.asarray(v, np.float32)
    return np.ascontiguousarray(v.reshape(v.shape[:-1] + (16, 128)).swapaxes(-1, -2))


def kernel(x_prompt, x_sample, c_prompt, c_sample, cache_kv_g0, cache_kv_g1, cache_kv_g2,
           state_conv, state_h, rel_bias, w_ada, b_ada, norm_g, w_in, conv_w, conv_b,
           w_r, b_r, w_i, b_i, lam, w_pa, w_pb, w_out, final_g, _depth=DEPTH):
    L = _depth
    if L not in _CACHE:
        _CACHE[L] = build(L)
    nc = _CACHE[L]
    bP, bC, bN = _gather_bias(rel_bias)
    f = lambda a: np.ascontiguousarray(np.asarray(a, np.float32))
    vecs = np.concatenate([
        np.asarray(b_ada, np.float32)[:L].reshape(L, 48, 128).swapaxes(1, 2),
        _fm(norm_g[:L]),
        _fm(conv_w[:L]).transpose(0, 2, 1, 3).reshape(L, 128, 64),
        _fm(conv_b[:L]), _fm(b_r[:L]), _fm(b_i[:L]), _fm(lam[:L])], axis=2)
    vecs = f(vecs)
    shared = {
        "biasP": f(bP.reshape(4, 128, -1)), "biasC": f(bC.reshape(4, 128, -1)), "biasN": f(bN.reshape(4, 4, -1)),
        "w_ada": f(w_ada[:L]), "w_in": f(w_in[:L]), "w_pa": f(w_pa[:L]), "w_pb": f(w_pb[:L]),
        "w_out": f(w_out[:L]), "w_r": f(w_r[:L]), "w_i": f(w_i[:L]), "vecs": vecs, "fing": _fm(final_g), "identin": np.eye(128, dtype=np.float32),
    }
    caches = (cache_kv_g0, cache_kv_g1, cache_kv_g2)
    in_maps = []
    for c in range(8):
        b = c % 4
        m = dict(shared)
        m["xin"] = f(np.concatenate([np.asarray(x_prompt[b], np.float32).T, np.asarray(x_sample[c], np.float32).T], axis=1))
        m["cvec"] = f(np.stack([_fm(c_prompt[b]), _fm(c_sample[c])], axis=-1))
        for g in range(3):
            m["cache%d" % g] = f(np.asarray(caches[g])[:L, c])
        m["sconv"] = f(_fm(np.asarray(state_conv)[:L, c]).transpose(0, 2, 3, 1))
        m["sh"] = _fm(np.asarray(state_h)[:L, c])
        in_maps.append(m)
    res = run_bass_kernel_spmd(nc, in_maps, core_ids=list(range(8)))
    R = res.results

    def unfm(a):
        return np.ascontiguousarray(np.swapaxes(a, -1, -2).reshape(a.shape[:-2] + (2048,)))
    y_prompt = np.stack([R[b]["yT"][:, :T].T for b in range(4)])
    y_sample = np.stack([R[c]["yT"][:, T:].T for c in range(8)])
    kvp = [np.stack([R[b]["kvp%d" % g] for b in range(4)], axis=1) for g in range(3)]
    kvs = [np.stack([R[c]["kvs%d" % g] for c in range(8)], axis=1) for g in range(3)]
    convp = np.stack([unfm(R[b]["convp"].transpose(0, 3, 1, 2)) for b in range(4)], axis=1)
    convs = np.stack([unfm(R[c]["convs"].transpose(0, 3, 1, 2)) for c in range(8)], axis=1)
    hp_ = np.stack([unfm(R[b]["hpo"]) for b in range(4)], axis=1)
    hs_ = np.stack([unfm(R[c]["hso"]) for c in range(8)], axis=1)
    outs = (y_prompt, y_sample, kvp[0], kvp[1], kvp[2], convp, hp_, kvs[0], kvs[1], kvs[2], convs, hs_)
    return tuple(np.ascontiguousarray(o, dtype=np.float32) for o in outs)
```

```python
import contextlib
import math
import numpy as np
import concourse.bass as bass
import concourse.mybir as mybir
from concourse.bass_utils import run_bass_kernel_spmd

F32 = mybir.dt.float32
BF16 = mybir.dt.bfloat16
AF = mybir.ActivationFunctionType
ALU = mybir.AluOpType

D = 2048
DEPTH = 4
T = 2048
TC = 1024
TS = 4
NTOK = T + TS
HD = 128
NB_BUCK = 32
IN_COLS = 18432
QO, KO, VO, ZAO, XLO, ZLO, GAO, GLO = 0, 3072, 6144, 9216, 10240, 12288, 14336, 16384
DIL = (1, 4, 16)
WB = (128, 512, 2048)
SCALE = HD ** -0.5
NEGM = -30000.0

_STEP_HOOK = None
ENGS = ("tensor", "vector", "scalar", "gpsimd", "sync")
SEM_EPOCH = 30000
NDMASEM = 6


def I(name, *a, **k):
    return (name, a, k)


class Sched:
    def __init__(self, nc, stack):
        self.nc = nc
        self.stack = stack
        self.nsem = 0
        self.lists = {e: [] for e in ENGS}
        self.cnt = {e: 0 for e in ENGS}
        self.sem = {e: self._newsem("s_" + e) for e in ENGS}
        self.waited = {e: {} for e in ENGS}
        self.lastw = {}
        self.readers = {}
        self.dsem = {e: [[self._newsem("d_%s%d" % (e, i)), 0] for i in range(NDMASEM)]
                     for e in ("sync", "gpsimd", "scalar")}
        self.drr = {e: 0 for e in ("sync", "gpsimd", "scalar")}

    def _newsem(self, name):
        self.nsem += 1
        return self.stack.enter_context(self.nc.semaphore(name + "_%d" % self.nsem))

    def _need(self, eng, ev, waits):
        if ev is None:
            return
        s, v = ev
        k = id(s)
        if self.waited[eng].get(k, 0) >= v:
            return
        self.waited[eng][k] = v
        waits.append((s, v))

    def _deps(self, eng, reads, writes):
        waits = []
        for b in reads:
            self._need(eng, self.lastw.get(b), waits)
        for b in writes:
            self._need(eng, self.lastw.get(b), waits)
            for ev in self.readers.get(b, ()):
                self._need(eng, ev, waits)
        return waits

    def _commit(self, ev, reads, writes):
        for b in reads:
            self.readers.setdefault(b, []).append(ev)
        for b in writes:
            self.lastw[b] = ev
            self.readers[b] = []

    def op(self, eng, fn, reads=(), writes=(), own_sync=True):
        pr = [b for b in reads if isinstance(b, tuple) and b[0] == "pb"]
        if pr:
            reads = [b for b in reads if not (isinstance(b, tuple) and b[0] == "pb")]
            writes = list(writes) + pr
        waits = self._deps(eng, reads, writes)
        if self.cnt[eng] >= SEM_EPOCH:
            self.sem[eng] = self._newsem("s_" + eng)
            self.cnt[eng] = 0
        self.cnt[eng] += 1
        ev = (self.sem[eng], self.cnt[eng])
        if not own_sync:
            waits = [w for w in waits if w[0] is not self.sem[eng]]
        self.lists[eng].append((waits, fn, (self.sem[eng], 1)))
        self._commit(ev, reads, writes)
        return ev

    def dma(self, eng, fn, reads=(), writes=()):
        waits = self._deps(eng, reads, writes)
        slot = self.dsem[eng][self.drr[eng] % NDMASEM]
        self.drr[eng] += 1
        s, v = slot
        if v > 0:
            self._need(eng, (s, v), waits)
        slot[1] = v + 16
        ev = (s, v + 16)
        self.lists[eng].append((waits, fn, (s, 16)))
        self._commit(ev, reads, writes)
        return ev

    def finish(self, final_events):
        waits = []
        for ev in final_events:
            self._need("sync", ev, waits)
        self.lists["sync"].append((waits, None, None))
        lists = self.lists

        def replay(e, items):
            for waits, fn, inc in items:
                for s, v in waits:
                    e.wait_ge(s, v)
                if fn is not None:
                    getattr(e, fn[0])(*fn[1], **fn[2]).then_inc(inc[0], inc[1])

        with self.nc.Block() as block:
            @block.tensor
            def _(e):
                replay(e, lists["tensor"])

            @block.vector
            def _(e):
                replay(e, lists["vector"])

            @block.scalar
            def _(e):
                replay(e, lists["scalar"])

            @block.gpsimd
            def _(e):
                replay(e, lists["gpsimd"])

            @block.sync
            def _(e):
                replay(e, lists["sync"])


def _t5_bucket(dist):
    dist = np.asarray(dist).astype(np.int32)
    max_exact = NB_BUCK // 2
    safe = np.maximum(dist, 1).astype(np.float32)
    large = max_exact + (np.log(safe / max_exact) / np.float32(math.log(2048 / max_exact))
                         * (NB_BUCK - max_exact)).astype(np.int32)
    large = np.minimum(large, NB_BUCK - 1)
    return np.where(dist < max_exact, dist, large).astype(np.int32)


def _bias_index_tables():
    P = -np.ones((3, 2, 128, 128), np.int64)
    k = np.arange(128)[:, None]
    q = np.arange(128)[None, :]
    for g in (0, 1):
        du = q - k
        P[g, 0] = np.where(du >= 0, _t5_bucket(np.clip(du, 0, None) * DIL[g]), -1)
        du = q + 128 - k
        P[g, 1] = np.where(du <= 128, _t5_bucket(du * DIL[g]), -1)
    rk, uk = k % 2, k // 2
    rq, uq = q % 2, q // 2
    du = uq - uk
    P[2, 0] = np.where((rk == rq) & (du >= 0), _t5_bucket(np.clip(du, 0, None) * 16), -1)
    du = uq + 64 - uk
    P[2, 1] = np.where(rk == rq, _t5_bucket(du * 16), -1)
    C = -np.ones((3, 4, 128), np.int64)
    N = -np.ones((3, 4, 4), np.int64)
    i = np.arange(128)
    for s in range(4):
        j = 128 + s - i
        C[0, s] = np.where(i >= s, _t5_bucket(np.clip(j, 0, 200)), -1)
        for g in (1, 2):
            C[g, s] = _t5_bucket((128 - i) * DIL[g])
        for kk in range(4):
            for g in range(3):
                dd = s - kk
                if dd >= 0 and dd % DIL[g] == 0:
                    N[g, kk, s] = _t5_bucket(dd)
    return P, C, N


def _gather_bias(rel_bias):
    P, C, N = _bias_index_tables()
    rb = np.concatenate([np.asarray(rel_bias, np.float32), np.full((1, 24), NEGM, np.float32)], 0)
    bP = np.zeros((4, 128, 3, 2, 2, 128), np.float32)
    bC = np.zeros((4, 128, 3, 4, 2), np.float32)
    bN = np.zeros((4, 4, 3, 2, 4), np.float32)
    for hp in range(4):
        for g in range(3):
            for hh in range(2):
                col = g * 8 + hp * 2 + hh
                for ty in range(2):
                    bP[hp, :, g, hh, ty, :] = rb[P[g, ty], col]
                for s in range(4):
                    bC[hp, :, g, s, hh] = rb[C[g, s], col]
                bN[hp, :, g, hh, :] = rb[N[g], col]
    return bP, bC, bN


def build(depth=DEPTH):
    nc = bass.Bass("TRN2", target_bir_lowering=False)

    def din(name, shape, dt=F32):
        return nc.dram_tensor(name, list(shape), dt, kind="ExternalInput").ap()

    def dout(name, shape, dt=F32):
        return nc.dram_tensor(name, list(shape), dt, kind="ExternalOutput").ap()

    L = depth
    xin = din("xin", [D, NTOK])
    cvec = din("cvec", [128, 16, 2])
    cache = [din("cache%d" % g, [L, WB[g], 2, 8, HD]) for g in range(3)]
    sconv = din("sconv", [L, 128, 16, 3])
    sh = din("sh", [L, 128, 16])
    biasP = din("biasP", [4, 128, 3 * 2 * 2 * 128])
    biasC = din("biasC", [4, 128, 3 * 4 * 2])
    biasN = din("biasN", [4, 4, 3 * 2 * 4])
    w_ada = din("w_ada", [L, D, 3 * D])
    w_in = din("w_in", [L, D, IN_COLS])
    w_pa = din("w_pa", [L, 1024, D])
    w_pb = din("w_pb", [L, D, D])
    w_out = din("w_out", [L, D, D])
    w_r = din("w_r", [L, 16, 128, 128])
    w_i = din("w_i", [L, 16, 128, 128])
    vecs = din("vecs", [L, 128, 48 + 16 * 9])
    fing = din("fing", [128, 16])
    identin = din("identin", [128, 128])

    yT = dout("yT", [D, NTOK])
    kvp = [dout("kvp%d" % g, [L, WB[g], 2, 8, HD]) for g in range(3)]
    convp = dout("convp", [L, 128, 16, 3])
    hpo = dout("hpo", [L, 128, 16])
    kvs = [dout("kvs%d" % g, [L, WB[g], 2, 8, HD]) for g in range(3)]
    convs = dout("convs", [L, 128, 16, 3])
    hso = dout("hso", [L, 128, 16])

    xres = nc.dram_tensor("xres", [D, NTOK], F32).ap()
    ctxK = nc.dram_tensor("ctxK", [3, 8, 128, 1024], BF16).ap()
    ctxV = nc.dram_tensor("ctxV", [3, 8, 128, 1024], BF16).ap()

    out_events = []
    st = contextlib.ExitStack()
    with st:
        S = Sched(nc, st)

        def sb(name, shape, dt):
            return st.enter_context(nc.sbuf_tensor(name, list(shape), dt))

        def pst(name, shape, dt=F32):
            return st.enter_context(nc.psum_tensor(name, list(shape), dt))

        NWB = 3
        wb = [sb("wb%d" % i, [128, 16 * 512], BF16) for i in range(NWB)]
        h = sb("h", [128, 16, TC + TS], BF16)
        att = sb("att", [128, 8, TC + TS], BF16)
        ylin = sb("ylin", [128, 16, TC + TS], BF16)
        REG = sb("regA", [128, 18432], BF16)
        ident = sb("ident", [128, 128], BF16)
        identf = sb("identf", [128, 128], F32)
        ones_f = sb("ones_f", [128, 128], F32)
        ones_b = sb("ones_b", [128, 128], BF16)
        csil = sb("csil", [128, 16, 2], BF16)
        cf = sb("cf", [128, 16, 2], F32)
        modb = [sb("mod%d" % i, [128, 48, 2], F32) for i in range(2)]
        gsb = [sb("gs%d" % i, [128, 16, 2], F32) for i in range(2)]
        vall = sb("vall", [128, L, 64], F32)
        vec = sb("vec", [128, 48 + 16 * 9], F32)
        lsc = sb("lsc", [128, 16], F32)
        lsc2 = sb("lsc2", [128, 16], F32)
        lsch = sb("lsch", [128, 32], F32)
        vech = sb("vech", [128, 32], F32)
        fg = sb("fg", [128, 16], F32)
        wrb = sb("wrb", [128, 16, 128], BF16)
        wib = sb("wib", [128, 16, 128], BF16)
        xld = [sb("xld%d" % i, [128, TC + TS], F32) for i in range(2)]
        bP = sb("bP", [128, 3, 2, 2, 128], F32)
        bC = sb("bC", [128, 3, 4, 2], F32)
        bN = sb("bN", [4, 3, 2, 4], F32)
        stage = [sb("stage%d" % i, [128, 256], F32) for i in range(3)]
        kvb = [sb("kvb%d" % i, [128, 256], BF16) for i in range(2)]
        pT = [sb("pT%d" % i, [128, 2, 256], BF16) for i in range(3)]
        halo = sb("halo", [128, 16, 3], F32)
        hcar = sb("hcar", [128, 16], F32)
        cvst = sb("cvst", [128, 16, 3], F32)
        cvss = sb("cvss", [128, 16, 3], F32)
        hsts = sb("hsts", [128, 16], F32)
        sh_t = sb("sh_t", [128, 16], F32)
        sconv_t = sb("sconv_t", [128, 16, 3], F32)
        ckt = [sb("ckt%d" % i, [128, 2, 2, 128], BF16) for i in range(4)]
        sKT = sb("sKT", [128, 4, 2, 128], BF16)
        qs = sb("qs", [128, 2, 4], BF16)
        sKn = sb("sKn", [4, 256], BF16)
        sVn = sb("sVn", [4, 256], BF16)
        sKnT = sb("sKnT", [128, 2, 4], BF16)
        sst = sb("sst", [4, 256], F32)
        spT = sb("spT", [128, 8], BF16)
        sptmp = sb("sptmp", [128, 8], F32)
        spn = sb("spn", [4, 2, 4], BF16)
        spntmp = sb("spntmp", [4, 2, 4], F32)
        sacc = sb("sacc", [128, 16], F32)
        dummy = sb("dummy_t", [128, 2], F32)
        if _STEP_HOOK is not None:
            print("SBUF remaining after alloc:", nc.sbuf_bytes_remaining)

        stmp = [xld[0][:, 0:512].rearrange("p (h c) -> p h c", h=2), xld[0][:, 512:1024].rearrange("p (h c) -> p h c", h=2),
                xld[1][:, 0:512].rearrange("p (h c) -> p h c", h=2)]
        def regv(off, n, dt):
            a = REG[:, off:off + n]
            return a.bitcast(F32) if dt == F32 else a
        sq = regv(0, 2 * (TC + TS), F32)
        rstd = regv(2100, 2 * (TC + TS), F32)
        sqb = [regv(4200, TC + TS, BF16), regv(5300, TC + TS, BF16)]
        sq2 = regv(6400, 2 * (TC + TS), F32)
        XR = [xld[0][:, :], xld[1][:, :]] + [regv(8500 + 2100 * i, 2 * (TC + TS), F32) for i in range(4)]
        XK = [("xld", 0), ("xld", 1), ("xr", 2), ("xr", 3), ("xr", 4), ("xr", 5)]
        QT = regv(0, 2 * 1024, BF16).rearrange("p (h t) -> p h t", h=2)
        KT = regv(2048, 2 * 1024, BF16).rearrange("p (h t) -> p h t", h=2)
        Vt = regv(4096, 8 * 256, BF16).rearrange("p (b c) -> p b c", b=8)
        cK = regv(6144, 2 * 1024, BF16).rearrange("p (h t) -> p h t", h=2)
        cV = regv(8192, 2 * 1024, BF16).rearrange("p (h b c) -> p h b c", h=2, b=8)
        acc = regv(10240, 4096, F32).rearrange("p (h t) -> p h t", h=2)
        den = regv(14336, 4096, F32).rearrange("p (h t) -> p h t", h=2)
        LW = TC + TS + 6
        xl = regv(0, 2 * LW, F32)
        xc = regv(2100, 2 * LW, F32)
        rt = regv(4200, 2 * LW, F32)
        it = regv(6300, 2 * LW, F32)
        a2 = regv(8400, 2 * LW, F32)
        gx = regv(10500, 2 * LW, F32)
        hst = regv(12600, 2 * LW, F32)
        zst = regv(14700, 2 * LW, F32)
        xcb = regv(16800, LW, BF16)
        merged = regv(0, 16 * (TC + TS), BF16).rearrange("p (c t) -> p c t", c=16)
        gat = regv(16448, 2 * 512, F32)
        REGKEYS = [(nm_, s_) for nm_ in ("xl", "xc", "rt", "it", "a2", "gx", "hst", "zst", "xcb") for s_ in (0, 1)] + \
                  ["QT", "KT", "Vt", "cK", "cV", "acc", "den", "xl", "xc", "rt", "it", "a2", "gx", "hst",
                   "zst", "xcb", "gat", "sq", "rstd", "sq2", ("sqb", 0), ("sqb", 1),
                   ("xld", 0), ("xld", 1), ("stmp", 0), ("stmp", 1), ("stmp", 2),
                   ("xr", 2), ("xr", 3), ("xr", 4), ("xr", 5)] + [("mg", m) for m in range(16)]

        PB = [pst("pb%d" % i, [128, 512]) for i in range(8)]
        PT7 = PB[7][:, :].bitcast(BF16)
        projrr = [0]

        def pbank():
            projrr[0] ^= 1
            return projrr[0]

        V = "vector"
        A = "scalar"
        PE = "tensor"

        def region_barrier():
            S.op(V, I("memset", dummy[:, 0:1], 0.0), writes=REGKEYS + ["dummy"])

        S.op(V, I("memset", ones_f[:], 1.0), writes=["ones_f"])
        S.op(V, I("memset", ones_b[:], 1.0), writes=["ones_b"])
        S.dma("sync", I("dma_start", out=identf[:], in_=identin), writes=["identf"])
        S.op(V, I("tensor_copy", ident[:], identf[:]), reads=["identf"], writes=["ident"])
        S.dma("sync", I("dma_start", out=cf[:], in_=cvec), writes=["cf"])
        S.dma("sync", I("dma_start", out=fg[:], in_=fing), writes=["fg"])
        for l_ in range(L):
            S.dma("sync", I("dma_start", out=vall[:, l_, :], in_=vecs[l_][:, 0:64]), writes=["vall"])
        S.op(A, I("activation", out=csil[:], in_=cf[:], func=AF.Silu), reads=["cf"], writes=["csil"])

        wstate = {"n": 0}

        def wload(pieces):
            bi = wstate["n"] % NWB
            wstate["n"] += 1
            off = 0
            views = []
            for (src, kc, ncols) in pieces:
                v = wb[bi][:, off:off + kc * ncols].rearrange("p (k n) -> p k n", k=kc)
                S.dma("gpsimd", I("dma_start",
                    out=v, in_=src.rearrange("(k p) n -> p k n", p=128)), writes=[("wb", bi)])
                views.append(v)
                off += kc * ncols
            assert off <= 16 * 512
            return bi, views

        def ntiles(nt):
            r = [(0, 512), (512, 512)]
            if nt > TC:
                r.append((TC, nt - TC))
            return r

        steps = []

        def add(pieces, fn):
            steps.append((pieces, fn))

        for l in range(L):
            xsrc = xin if l == 0 else xres

            def layer_prologue(bi, views, l=l):
                S.dma("sync", I("dma_start", out=vec[:], in_=vecs[l]), writes=["vec"])
                S.dma("gpsimd", I("dma_start", out=wrb[:], in_=w_r[l].rearrange("n c d -> c n d")),
                      writes=["wrb"])
                S.dma("gpsimd", I("dma_start", out=wib[:], in_=w_i[l].rearrange("n c d -> c n d")),
                      writes=["wib"])
                S.dma("sync", I("dma_start", out=sh_t[:], in_=sh[l]), writes=["sh_t"])
                S.dma("sync", I("dma_start", out=sconv_t[:], in_=sconv[l]), writes=["sconv_t"])
                lam_v = vec[:, 48 + 16 * 8: 48 + 16 * 9]
                S.op(A, I("activation", out=lsc[:], in_=lam_v, func=AF.Exp, scale=-1.0),
                     reads=["vec"], writes=["lsc"])
                S.op(A, I("activation", out=lsc[:], in_=lsc[:], func=AF.Ln, bias=1.0, scale=1.0),
                     reads=["lsc"], writes=["lsc"])
                S.op(V, I("tensor_scalar", lsc2[:], lsc[:], -16.0, None, ALU.mult),
                     reads=["lsc"], writes=["lsc2"])
                S.op(V, I("tensor_scalar", lsc[:], lsc[:], -8.0, None, ALU.mult),
                     reads=["lsc"], writes=["lsc"])
                S.op(V, I("tensor_scalar", lsch[:, 0:16], lsc[:], 0.5, None, ALU.mult), reads=["lsc"], writes=["lsch"])
                S.op(V, I("tensor_scalar", lsch[:, 16:32], lsc2[:], 0.5, None, ALU.mult), reads=["lsc2", "lsch"], writes=["lsch"])
                S.op(V, I("tensor_scalar", vech[:, :], vec[:, 144:176], 0.5, None, ALU.mult), reads=["vec"], writes=["vech"])
                for g in range(3):
                    n_el = (WB[g] - TS) * 2048
                    src = cache[g][l].rearrange("w a h d -> (w a h d)")[TS * 2048: WB[g] * 2048]
                    dst = kvs[g][l].rearrange("w a h d -> (w a h d)")[0: n_el]
                    ev = S.dma("sync", I("dma_start",
                        out=dst.rearrange("(p n) -> p n", p=128), in_=src.rearrange("(p n) -> p n", p=128)))
                    out_events.append(ev)
            add(None, layer_prologue)

            def make_ada(l, pc):
                mod = modb[l % 2]
                gs = gsb[l % 2]
                mk, gk = ("mod", l % 2), ("gs", l % 2)

                def ada_step(bi, views, l=l, pc=pc):
                    wv = views[0]
                    for mi in range(4):
                        m = pc * 4 + mi
                        b = pbank()
                        for k in range(16):
                            S.op(PE, I("matmul", PB[b][:, 0:2], lhsT=wv[:, k, mi * 128:(mi + 1) * 128], rhs=csil[:, k, :],
                                       start=(k == 0), stop=(k == 15)),
                                 reads=[("wb", bi), "csil"], writes=[("pb", b)], own_sync=False)
                        S.op(V, I("tensor_scalar", mod[:, m, :], PB[b][:, 0:2], vall[:, l, m:m + 1], None, ALU.add),
                             reads=[("pb", b), "vall"], writes=[mk])
                    if pc == 11:
                        for j in range(2):
                            S.op(V, I("scalar_tensor_tensor", gs[:, :, j], mod[:, 16:32, j], 1.0, vall[:, l, 48:64],
                                      ALU.add, ALU.mult),
                                 reads=[mk, "vall"], writes=[gk])
                return ([(w_ada[l][:, pc * 512:(pc + 1) * 512], 16, 512)], ada_step)

            if l == 0:
                for pc in range(12):
                    add(*make_ada(0, pc))
            mod = modb[l % 2]
            gs = gsb[l % 2]
            MK, GK = ("mod", l % 2), ("gs", l % 2)

            for ch in range(2):
                t0 = ch * TC
                nt = TC + (TS if ch == 1 else 0)
                has_s = ch == 1
                has_ctx = ch == 1
                NT = ntiles(nt)
                segs = [(0, TC, 0)] + ([(TC, TS, 1)] if has_s else [])

                def xsl(k, c0, n, xsrc=xsrc, t0=t0):
                    if c0 >= TC:
                        return xsrc[k * 128:(k + 1) * 128, T + (c0 - TC): T + (c0 - TC) + n]
                    return xsrc[k * 128:(k + 1) * 128, t0 + c0: t0 + c0 + n]

                def load_x(k, buf, nt=nt, xsl=xsl, has_s=has_s):
                    S.dma("sync", I("dma_start", out=XR[buf][:, 0:TC], in_=xsl(k, 0, TC)),
                          reads=[("xres", k)], writes=[XK[buf]])
                    if has_s:
                        S.dma("sync", I("dma_start", out=XR[buf][:, TC:nt], in_=xsl(k, TC, TS)),
                              reads=[("xres", k)], writes=[XK[buf]])

                def norm_step(bi, views, nt=nt, NT=NT, segs=segs, load_x=load_x, mod=mod, gs=gs, MK=MK, GK=GK):
                    region_barrier()
                    NR = 6
                    for i0 in range(NR - 1):
                        load_x(i0 % 16, i0 % NR)
                    for k in range(16):
                        i_ = k
                        if i_ + NR - 1 < 32:
                            load_x((i_ + NR - 1) % 16, (i_ + NR - 1) % NR)
                        xb, xk = XR[i_ % NR], XK[i_ % NR]
                        S.op(A, I("activation", out=sqb[k % 2][:, 0:nt], in_=xb[:, 0:nt], func=AF.Square),
                             reads=[xk], writes=[("sqb", k % 2)])
                        for ti, (c0, n) in enumerate(NT):
                            S.op(PE, I("matmul", PB[4 + ti][:, 0:n], lhsT=ones_b[:], rhs=sqb[k % 2][:, c0:c0 + n],
                                       start=(k == 0), stop=(k == 15)),
                                 reads=[("sqb", k % 2), "ones_b"], writes=[("pb", 4 + ti)], own_sync=False)
                    for ti, (c0, n) in enumerate(NT):
                        S.op(V, I("tensor_scalar", rstd[:, c0:c0 + n], PB[4 + ti][:, 0:n], 1.0 / D, 1e-6, ALU.mult, ALU.add),
                             reads=[("pb", 4 + ti)], writes=["rstd"])
                    S.op(A, I("activation", out=rstd[:, 0:nt], in_=rstd[:, 0:nt], func=AF.Sqrt),
                         reads=["rstd"], writes=["rstd"])
                    S.op(V, I("reciprocal", rstd[:, 0:nt], rstd[:, 0:nt]),
                         reads=["rstd"], writes=["rstd"])
                    for k in range(16):
                        i_ = 16 + k
                        if i_ + NR - 1 < 32:
                            load_x((i_ + NR - 1) % 16, (i_ + NR - 1) % NR)
                        xb, xk = XR[i_ % NR], XK[i_ % NR]
                        sqt, sqk = (sq, "sq") if k % 2 == 0 else (sq2, "sq2")
                        S.op(V, I("tensor_tensor", sqt[:, 0:nt], xb[:, 0:nt], rstd[:, 0:nt], ALU.mult),
                             reads=[xk, "rstd"], writes=[sqk])
                        for (c0, n, j) in segs:
                            S.op(A, I("activation", out=h[:, k, c0:c0 + n], in_=sqt[:, c0:c0 + n], func=AF.Identity,
                                      scale=gs[:, k, j:j + 1], bias=mod[:, k, j:j + 1]),
                                 reads=[sqk, GK, MK], writes=[("h", k)])
                add(None, norm_step)

                HK = [("h", k) for k in range(16)]

                for hp in range(4):
                    for g in range(3):
                        dil = DIL[g]

                        def perm_view(ap2, q0, dil=dil):
                            if dil == 1:
                                return ap2[:, q0 * 256:(q0 + 1) * 256]
                            if dil == 4:
                                return ap2.rearrange("p (u r) -> p r u", r=4)[:, q0, :]
                            return ap2.rearrange("p (u r) -> p r u", r=8)[:, q0 * 2:(q0 + 1) * 2, :]

                        def blk_tokens(ap3, b, dil=dil):
                            if dil == 1:
                                return ap3[:, b * 128:(b + 1) * 128]
                            if dil == 4:
                                return ap3.rearrange("p (u r) -> p r u", r=4)[:, b // 2, (b % 2) * 128:(b % 2 + 1) * 128]
                            return ap3.rearrange("p (u r) -> p r u", r=16)[:, 2 * b:2 * b + 2, :]

                        def out_rows(b, g=g, dil=dil, t0=t0):
                            lo = T - WB[g]
                            res = []
                            if dil == 1:
                                tk = t0 + b * 128
                                if tk >= lo:
                                    res.append((0, 128, tk - lo, 1))
                            elif dil == 4:
                                r, n = b // 2, b % 2
                                tk = t0 + 4 * (n * 128) + r
                                if tk >= lo:
                                    res.append((0, 128, tk - lo, 4))
                            else:
                                for rl in range(2):
                                    res.append((rl * 64, 64, t0 + 2 * b + rl, 16))
                            return res

                        def gen_blocks(g=g, dil=dil, t0=t0, has_s=has_s):
                            lo = T - WB[g]
                            res = []
                            if dil == 16:
                                for r in range(8):
                                    res.append(dict(kind="p", i=r, M=128, poff=0, tile=r, pos0=r * 128,
                                                    lhs=(lambda k, r=r: h[:, k, r:TC:8]), row0=t0 + r, rs=8))
                            else:
                                for bk in range(8):
                                    if dil == 1:
                                        tk, rs = t0 + bk * 128, 1
                                        lhs = (lambda k, bk=bk: h[:, k, bk * 128:(bk + 1) * 128])
                                    else:
                                        r, n = bk // 2, bk % 2
                                        tk, rs = t0 + 512 * n + r, 4
                                        lhs = (lambda k, r=r, n=n: h[:, k, n * 512 + r:(n + 1) * 512:4])
                                    res.append(dict(kind="p", i=bk, M=128, poff=0, tile=bk, pos0=bk * 128, lhs=lhs,
                                                    row0=(tk - lo) if tk >= lo else None, rs=rs))
                            if has_s:
                                res.append(dict(kind="s", i=99, M=TS, lhs=(lambda k: h[:, k, TC:TC + TS])))
                            return res

                        def stepA(bi, views, l=l, hp=hp, g=g, nt=nt, NT=NT, has_s=has_s, ch=ch, dil=dil,
                                  perm_view=perm_view, gen_blocks=gen_blocks):
                            wq, wk = views
                            if g == 0 and hp == 0:
                                region_barrier()
                            if ch == 1:
                                for hh in range(2):
                                    S.dma("sync", I("dma_start", out=cK[:, hh, :], in_=ctxK[g, hp * 2 + hh]),
                                          reads=[("ctxK", g, hp * 2 + hh)], writes=["cK"])
                                for s_ in range(TS):
                                    csrc = cache[g][l][s_::dil, :, hp * 2:hp * 2 + 2, :] if dil > 1 else \
                                        cache[g][l][:, :, hp * 2:hp * 2 + 2, :]
                                    S.dma("gpsimd", I("dma_start", out=ckt[s_][:], in_=csrc), writes=[("ckt", s_)])
                            if g == 0:
                                S.dma("sync", I("dma_start",
                                    out=bP[:].rearrange("p a b c d -> p (a b c d)"), in_=biasP[hp]), writes=["bP"])
                                if has_s:
                                    S.dma("sync", I("dma_start",
                                        out=bC[:].rearrange("p a b c -> p (a b c)"), in_=biasC[hp]), writes=["bC"])
                                    S.dma("sync", I("dma_start",
                                        out=bN[:].rearrange("p a b c -> p (a b c)"), in_=biasN[hp]), writes=["bN"])
                            for hh in range(2):
                                for (c0, n) in NT:
                                    b = pbank()
                                    for k in range(16):
                                        S.op(PE, I("matmul",
                                            PB[b][:, 0:n], lhsT=wq[:, k, hh * 128:(hh + 1) * 128],
                                            rhs=h[:, k, c0:c0 + n], start=(k == 0), stop=(k == 15)),
                                            reads=[("wb", bi), ("h", k)], writes=[("pb", b)], own_sync=False)
                                    if c0 >= TC:
                                        S.op(A, I("copy", qs[:, hh, :], PB[b][:, 0:n]),
                                             reads=[("pb", b)], writes=["qs"])
                                    else:
                                        if DIL[g] == 1:
                                            S.op(A, I("copy",
                                                QT[:, hh, c0:c0 + 512], PB[b][:, 0:512]),
                                                reads=[("pb", b)], writes=["QT"])
                                        else:
                                            r_ = 8 if DIL[g] == 16 else DIL[g]
                                            nu = 512 // r_
                                            u0 = c0 // r_
                                            dst = QT[:, hh, :].rearrange("p (r u) -> p r u", r=r_)[:, :, u0:u0 + nu]
                                            src = PB[b][:, 0:512].rearrange("p (u r) -> p r u", r=r_)
                                            S.op(A, I("copy", dst, src),
                                                 reads=[("pb", b)], writes=["QT"])
                            pending = [None]
                            for gb in gen_blocks():
                                b = pbank()
                                M = gb["M"]
                                for k in range(16):
                                    S.op(PE, I("matmul", PB[b][0:M, 0:256], lhsT=gb["lhs"](k), rhs=wk[:, k, :],
                                               start=(k == 0), stop=(k == 15)),
                                         reads=[("wb", bi), ("h", k)], writes=[("pb", b)], own_sync=False)
                                if gb["kind"] == "s":
                                    if pending[0] is not None:
                                        pending[0]()
                                        pending[0] = None
                                    S.op(A, I("copy", sst[:, :], PB[b][0:TS, 0:256]),
                                         reads=[("pb", b)], writes=["sst"])
                                    S.op(V, I("tensor_copy", sKn[:, :], PB[b][0:TS, 0:256]),
                                         reads=[("pb", b)], writes=["sKn"])
                                    dst = kvs[g][l][WB[g] - TS:WB[g], 0, hp * 2:hp * 2 + 2, :]
                                    ev = S.dma("sync", I("dma_start", out=dst,
                                                         in_=sst[:, :].rearrange("p (h d) -> p h d", h=2)), reads=["sst"])
                                    out_events.append(ev)
                                    for hh in range(2):
                                        S.op(PE, I("transpose", PT7[:, hh * 4:hh * 4 + 4],
                                                   sKn[:, hh * 128:(hh + 1) * 128], ident[0:TS, 0:TS]),
                                             reads=["sKn", "ident"], writes=[("pb", 7)])
                                    S.op(V, I("tensor_copy", sKnT[:].rearrange("p a b -> p (a b)"), PT7[:, 0:8]),
                                         reads=[("pb", 7)], writes=["sKnT"])
                                    continue
                                gi = gb["i"]
                                kb_i = gi % 2
                                if gb["row0"] is not None:
                                    si = gi % 3
                                    S.op(A, I("copy", stage[si][0:M, :], PB[b][0:M, 0:256]),
                                         reads=[("pb", b)], writes=[("stage", si)])
                                    r0, rs = gb["row0"], gb["rs"]
                                    dst = kvp[g][l][r0:r0 + (M - 1) * rs + 1:rs, 0, hp * 2:hp * 2 + 2, :]
                                    ev = S.dma("sync", I("dma_start", out=dst,
                                                         in_=stage[si][0:M, :].rearrange("p (h d) -> p h d", h=2)),
                                               reads=[("stage", si)])
                                    out_events.append(ev)
                                S.op(V, I("tensor_copy", kvb[kb_i][0:M, :], PB[b][0:M, 0:256]),
                                     reads=[("pb", b)], writes=[("kvb", kb_i)])
                                if pending[0] is not None:
                                    pending[0]()

                                def do_tr(M=M, kb_i=kb_i, pos0=gb["pos0"]):
                                    for hh in range(2):
                                        c_ = hh * 512 + (pos0 % 512)
                                        S.op(PE, I("transpose", PT7[:, c_:c_ + M], kvb[kb_i][0:M, hh * 128:(hh + 1) * 128],
                                                   ident[0:M, 0:M]),
                                             reads=[("kvb", kb_i), "ident"], writes=[("pb", 7)])
                                    if (pos0 + M) % 512 == 0:
                                        p0_ = pos0 + M - 512
                                        for hh in range(2):
                                            S.op(A, I("copy", KT[:, hh, p0_:p0_ + 512], PT7[:, hh * 512:(hh + 1) * 512]),
                                                 reads=[("pb", 7)], writes=["KT"])
                                pending[0] = do_tr
                            if pending[0] is not None:
                                pending[0]()
                                pending[0] = None
                            if ch == 0:
                                for hh in range(2):
                                    S.dma("sync", I("dma_start",
                                        out=ctxK[g, hp * 2 + hh], in_=KT[:, hh, :]), reads=["KT"],
                                        writes=[("ctxK", g, hp * 2 + hh)])

                        add([(w_in[l][:, QO + g * 1024 + hp * 256: QO + g * 1024 + hp * 256 + 256], 16, 256),
                             (w_in[l][:, KO + g * 1024 + hp * 256: KO + g * 1024 + hp * 256 + 256], 16, 256)], stepA)

                        def stepB(bi, views, l=l, hp=hp, g=g, nt=nt, NT=NT, has_s=has_s, has_ctx=has_ctx, ch=ch,
                                  perm_view=perm_view, gen_blocks=gen_blocks, dil=dil):
                            wv = views[0]
                            if ch == 1:
                                for hh in range(2):
                                    S.dma("sync", I("dma_start", out=cV[:, hh, :, :],
                                                    in_=ctxV[g, hp * 2 + hh].rearrange("p (b c) -> p b c", b=8)),
                                          reads=[("ctxV", g, hp * 2 + hh)], writes=["cV"])
                            units = []
                            if dil == 1:
                                if has_ctx:
                                    units.append(("c", 7, [(0, 1)]))
                                for kb in range(8):
                                    units.append(("k", kb, [(kb, 0)] + ([(kb + 1, 1)] if kb < 7 else [])))
                            elif dil == 4:
                                for r in range(4):
                                    if has_ctx:
                                        units.append(("c", 2 * r + 1, [(2 * r, 1)]))
                                    units.append(("k", 2 * r, [(2 * r, 0), (2 * r + 1, 1)]))
                                    units.append(("k", 2 * r + 1, [(2 * r + 1, 0)]))
                            else:
                                for m in range(8):
                                    if has_ctx:
                                        units.append(("c", m, [(m, 1)]))
                                    units.append(("k", m, [(m, 0)]))
                            last_unit_of_q = {}
                            for ui, (src, kb, qs_) in enumerate(units):
                                for (qb, ty) in qs_:
                                    last_unit_of_q[qb] = ui
                            SRING = [2, 3]
                            AHEAD = 2

                            def emit_S(ui):
                                src, kb, qs_ = units[ui]
                                nq = len(qs_)
                                q0 = qs_[0][0]
                                sbk = SRING[ui % 2]
                                ti = ui % 3
                                for hh in range(2):
                                    ktile = (KT if src == "k" else cK)[:, hh, kb * 128:(kb + 1) * 128]
                                    S.op(PE, I("matmul", PB[sbk][:, hh * 256:hh * 256 + nq * 128], lhsT=ktile,
                                               rhs=QT[:, hh, q0 * 128:(q0 + nq) * 128], start=True, stop=True,
                                               skip_group_check=True),
                                         reads=["KT" if src == "k" else "cK", "QT"], writes=[("pb", sbk)], own_sync=False)
                                ty0 = qs_[0][1]
                                bias_ap = bP[:, g, :, ty0:ty0 + nq, :].rearrange("p h a b -> p h (a b)")
                                psv = PB[sbk][:, :].rearrange("p (h c) -> p h c", h=2)[:, :, 0:nq * 128]
                                S.op(V, I("scalar_tensor_tensor", stmp[ti][:, :, 0:nq * 128], psv, SCALE, bias_ap,
                                          ALU.mult, ALU.add),
                                     reads=[("pb", sbk), "bP"], writes=[("stmp", ti)])
                                S.op(A, I("activation", out=pT[ti][:, :, 0:nq * 128], in_=stmp[ti][:, :, 0:nq * 128],
                                          func=AF.Exp),
                                     reads=[("stmp", ti)], writes=[("pT", ti)])

                            for ui in range(min(AHEAD, len(units))):
                                emit_S(ui)
                            for gb in gen_blocks():
                                b = pbank()
                                M = gb["M"]
                                for k in range(16):
                                    S.op(PE, I("matmul", PB[b][0:M, 0:256], lhsT=gb["lhs"](k), rhs=wv[:, k, 0:256],
                                               start=(k == 0), stop=(k == 15)),
                                         reads=[("wb", bi), ("h", k)], writes=[("pb", b)], own_sync=False)
                                if gb["kind"] == "s":
                                    S.op(A, I("copy", sst[:, :], PB[b][0:TS, 0:256]),
                                         reads=[("pb", b)], writes=["sst"])
                                    S.op(V, I("tensor_copy", sVn[:, :], PB[b][0:TS, 0:256]),
                                         reads=[("pb", b)], writes=["sVn"])
                                    dst = kvs[g][l][WB[g] - TS:WB[g], 1, hp * 2:hp * 2 + 2, :]
                                    ev = S.dma("sync", I("dma_start", out=dst,
                                                         in_=sst[:, :].rearrange("p (h d) -> p h d", h=2)), reads=["sst"])
                                    out_events.append(ev)
                                    continue
                                gi = gb["i"]
                                if gb["row0"] is not None:
                                    si = gi % 3
                                    S.op(A, I("copy", stage[si][0:M, :], PB[b][0:M, 0:256]),
                                         reads=[("pb", b)], writes=[("stage", si)])
                                    r0, rs = gb["row0"], gb["rs"]
                                    dst = kvp[g][l][r0:r0 + (M - 1) * rs + 1:rs, 1, hp * 2:hp * 2 + 2, :]
                                    ev = S.dma("sync", I("dma_start", out=dst,
                                                         in_=stage[si][0:M, :].rearrange("p (h d) -> p h d", h=2)),
                                               reads=[("stage", si)])
                                    out_events.append(ev)
                                tile_, poff = gb["tile"], gb["poff"]
                                if poff == 0:
                                    S.op(V, I("tensor_copy", Vt[0:M, tile_, :], PB[b][0:M, 0:256]),
                                         reads=[("pb", b)], writes=["Vt"])
                                else:
                                    kb_i = gi % 2
                                    S.op(V, I("tensor_copy", kvb[kb_i][0:M, :], PB[b][0:M, 0:256]),
                                         reads=[("pb", b)], writes=[("kvb", kb_i)])
                                    S.dma("sync", I("dma_start", out=Vt[poff:poff + M, tile_, :], in_=kvb[kb_i][0:M, :]),
                                          reads=[("kvb", kb_i)], writes=["Vt"])
                            if ch == 0:
                                for hh in range(2):
                                    S.dma("sync", I("dma_start",
                                        out=ctxV[g, hp * 2 + hh].rearrange("p (b c) -> p b c", b=8),
                                        in_=Vt[:, :, hh * 128:(hh + 1) * 128]), reads=["Vt"],
                                        writes=[("ctxV", g, hp * 2 + hh)])
                            fresh = {}
                            done_q = set()
                            for ui, (src, kb, qs_) in enumerate(units):
                                if ui + AHEAD < len(units):
                                    emit_S(ui + AHEAD)
                                nq = len(qs_)
                                ti = ui % 3
                                for hh in range(2):
                                    vtile = (Vt[:, kb, hh * 128:(hh + 1) * 128] if src == "k" else cV[:, hh, kb, :])
                                    for qi, (qb, ty) in enumerate(qs_):
                                        qt = qb // 2
                                        ob = 4 + 2 * (qt % 2)
                                        first = fresh.get(qt, True)
                                        fresh[qt] = False
                                        oc = hh * 256 + (qb % 2) * 128
                                        S.op(PE, I("matmul", PB[ob][:, oc:oc + 128], lhsT=vtile,
                                                   rhs=pT[ti][:, hh, qi * 128:(qi + 1) * 128],
                                                   start=first, stop=False, skip_group_check=True),
                                             reads=["Vt" if src == "k" else "cV", ("pT", ti)], writes=[("pb", ob)],
                                             own_sync=False)
                                        S.op(PE, I("matmul", PB[ob + 1][:, oc:oc + 128], lhsT=ones_b[:],
                                                   rhs=pT[ti][:, hh, qi * 128:(qi + 1) * 128],
                                                   start=first, stop=False, skip_group_check=True),
                                             reads=["ones_b", ("pT", ti)], writes=[("pb", ob + 1)], own_sync=False)
                                for qt in sorted(fresh.keys()):
                                    if qt in done_q:
                                        continue
                                    if last_unit_of_q[2 * qt] > ui or last_unit_of_q[2 * qt + 1] > ui:
                                        continue
                                    done_q.add(qt)
                                    ob = 4 + 2 * (qt % 2)
                                    if dil == 16:
                                        pairs = []
                                        for hh in range(2):
                                            pairs.append((perm_view(acc[:, hh, :], qt), perm_view(den[:, hh, :], qt),
                                                          PB[ob][:, hh * 256:(hh + 1) * 256].rearrange("p (r u) -> p r u", r=2),
                                                          PB[ob + 1][:, hh * 256:(hh + 1) * 256].rearrange("p (r u) -> p r u", r=2)))
                                    else:
                                        if dil == 1:
                                            av = acc[:, :, qt * 256:(qt + 1) * 256]
                                            dv = den[:, :, qt * 256:(qt + 1) * 256]
                                        else:
                                            av = acc[:, :, :].rearrange("p h (u r) -> p h r u", r=4)[:, :, qt, :]
                                            dv = den[:, :, :].rearrange("p h (u r) -> p h r u", r=4)[:, :, qt, :]
                                        pairs = [(av, dv, PB[ob][:, :].rearrange("p (h c) -> p h c", h=2),
                                                  PB[ob + 1][:, :].rearrange("p (h c) -> p h c", h=2))]
                                    for (av, dv, osrc, dsrc) in pairs:
                                        if g == 0:
                                            S.op(A, I("copy", av, osrc), reads=[("pb", ob)], writes=["acc"])
                                            S.op(V, I("tensor_copy", dv, dsrc), reads=[("pb", ob + 1)], writes=["den"])
                                        else:
                                            S.op(V, I("tensor_tensor", av, av, osrc, ALU.add),
                                                 reads=[("pb", ob), "acc"], writes=["acc"])
                                            S.op(V, I("tensor_tensor", dv, dv, dsrc, ALU.add),
                                                 reads=[("pb", ob + 1), "den"], writes=["den"])
                            if has_s:
                                for s in range(TS):
                                    for hh in range(2):
                                        c_ = (s * 2 + hh) * 128
                                        S.op(PE, I("transpose", PT7[:, c_:c_ + 128], ckt[s][:, 0, hh, :], ident[:]),
                                             reads=[("ckt", s), "ident"], writes=[("pb", 7)])
                                S.op(A, I("copy", sKT[:].rearrange("p s a b -> p (s a b)"), PT7[:, 0:1024]),
                                     reads=[("pb", 7)], writes=["sKT"])
                                sbk = 3
                                for s in range(TS):
                                    for hh in range(2):
                                        S.op(PE, I("matmul", PB[sbk][:, s * 2 + hh:s * 2 + hh + 1], lhsT=sKT[:, s, hh, :],
                                                   rhs=qs[:, hh, s:s + 1], start=True, stop=True, skip_group_check=True),
                                             reads=["sKT", "qs"], writes=[("pb", sbk)], own_sync=False)
                                S.op(V, I("scalar_tensor_tensor", sptmp[:, :], PB[sbk][:, 0:8], SCALE,
                                          bC[:, g, :, :].rearrange("p a b -> p (a b)"), ALU.mult, ALU.add),
                                     reads=[("pb", sbk), "bC"], writes=["sptmp"])
                                S.op(A, I("activation", out=spT[:, :], in_=sptmp[:, :], func=AF.Exp),
                                     reads=["sptmp"], writes=["spT"])
                                for s in range(TS):
                                    for hh in range(2):
                                        first = (s == 0 and hh == 0)
                                        S.op(PE, I("matmul", PB[6][:, hh * 4 + s:hh * 4 + s + 1], lhsT=ckt[s][:, 1, hh, :],
                                                   rhs=spT[:, s * 2 + hh:s * 2 + hh + 1], start=first, stop=False,
                                                   skip_group_check=True),
                                             reads=[("ckt", s), "spT"], writes=[("pb", 6)], own_sync=False)
                                        S.op(PE, I("matmul", PB[6][:, 8 + hh * 4 + s:8 + hh * 4 + s + 1], lhsT=ones_b[:],
                                                   rhs=spT[:, s * 2 + hh:s * 2 + hh + 1], start=False, stop=False,
                                                   skip_group_check=True),
                                             reads=["ones_b", "spT"], writes=[("pb", 6)], own_sync=False)
                                sbk = 2
                                for hh in range(2):
                                    S.op(PE, I("matmul",
                                        PB[sbk][0:TS, hh * 4:hh * 4 + 4], lhsT=sKnT[:, hh, :], rhs=qs[:, hh, :],
                                        start=True, stop=True, skip_group_check=True),
                                        reads=["sKnT", "qs"], writes=[("pb", sbk)], own_sync=False)
                                S.op(V, I("scalar_tensor_tensor",
                                    spntmp[:].rearrange("p a b -> p (a b)"), PB[sbk][0:TS, 0:8], SCALE,
                                    bN[:, g, :, :].rearrange("p a b -> p (a b)"), ALU.mult, ALU.add),
                                    reads=[("pb", sbk), "bN"], writes=["spntmp"])
                                S.op(A, I("activation", out=spn[:].rearrange("p a b -> p (a b)"),
                                                               in_=spntmp[:].rearrange("p a b -> p (a b)"), func=AF.Exp),
                                     reads=["spntmp"], writes=["spn"])
                                for hh in range(2):
                                    S.op(PE, I("matmul",
                                        PB[6][:, hh * 4:hh * 4 + 4], lhsT=sVn[:, hh * 128:(hh + 1) * 128],
                                        rhs=spn[:, hh, :], start=False, stop=False, skip_group_check=True),
                                        reads=["sVn", "spn"], writes=[("pb", 6)], own_sync=False)
                                    S.op(PE, I("matmul",
                                        PB[6][:, 8 + hh * 4:8 + hh * 4 + 4], lhsT=ones_b[0:TS, :],
                                        rhs=spn[:, hh, :], start=False, stop=False, skip_group_check=True),
                                        reads=["ones_b", "spn"], writes=[("pb", 6)], own_sync=False)
                            if has_s:
                                if g == 0:
                                    S.op(V, I("tensor_copy", sacc[:, :], PB[6][:, 0:16]),
                                         reads=[("pb", 6)], writes=["sacc"])
                                else:
                                    S.op(V, I("tensor_tensor", sacc[:, :], sacc[:, :], PB[6][:, 0:16], ALU.add),
                                         reads=[("pb", 6), "sacc"], writes=["sacc"])
                            if g == 2:
                                wz = views[1]
                                if has_s:
                                    S.op(V, I("reciprocal", sacc[:, 8:16], sacc[:, 8:16]),
                                         reads=["sacc"], writes=["sacc"])
                                    S.op(V, I("tensor_tensor", sacc[:, 0:8], sacc[:, 0:8], sacc[:, 8:16], ALU.mult),
                                         reads=["sacc"], writes=["sacc"])
                                for hh in range(2):
                                    S.op(V, I("reciprocal", den[:, hh, :], den[:, hh, :]),
                                         reads=["den"], writes=["den"])
                                    S.op(V, I("tensor_tensor", acc[:, hh, :], acc[:, hh, :], den[:, hh, :],
                                                                             ALU.mult), reads=["den", "acc"], writes=["acc"])
                                    for (c0, n) in NT:
                                        b = pbank()
                                        for k in range(16):
                                            S.op(PE, I("matmul",
                                                PB[b][:, 0:n], lhsT=wz[:, k, hh * 128:(hh + 1) * 128],
                                                rhs=h[:, k, c0:c0 + n], start=(k == 0), stop=(k == 15)),
                                                reads=[("wb", bi), ("h", k)], writes=[("pb", b)], own_sync=False)
                                        ti = b
                                        S.op(A, I("activation",
                                            out=stmp[ti][:, 0, 0:n] if n <= 256 else den[:, hh, c0:c0 + n],
                                            in_=PB[b][:, 0:n], func=AF.Silu),
                                            reads=[("pb", b)], writes=[("stmp", ti), "den"])
                                        if c0 >= TC:
                                            S.op(V, I("tensor_tensor",
                                                att[:, hp * 2 + hh, c0:c0 + n], sacc[:, hh * 4:hh * 4 + 4],
                                                stmp[ti][:, 0, 0:n], ALU.mult),
                                                reads=["sacc", ("stmp", ti)], writes=[("att", hp * 2 + hh)])
                                        else:
                                            S.op(V, I("tensor_tensor",
                                                att[:, hp * 2 + hh, c0:c0 + n], acc[:, hh, c0:c0 + n],
                                                den[:, hh, c0:c0 + n], ALU.mult),
                                                reads=["acc", "den"], writes=[("att", hp * 2 + hh)])

                        pcs = [(w_in[l][:, VO + g * 1024 + hp * 256: VO + g * 1024 + hp * 256 + 256], 16, 256)]
                        if g == 2:
                            pcs.append((w_in[l][:, ZAO + hp * 256: ZAO + hp * 256 + 256], 16, 256))
                        add(pcs, stepB)

                for np_ in range(8):
                    def lru_step(bi, views, l=l, np_=np_, nt=nt, NT=NT, has_s=has_s, ch=ch, segs=segs):
                        wx, wz = views
                        if np_ == 0:
                            region_barrier()
                        for ni in range(2):
                            n_ = np_ * 2 + ni
                            W_ = nt + 3 * len(segs) - 3
                            SEG = [(0, 512), (512, W_)]
                            if ch == 0:
                                S.op(V, I("memset", xl[:, 0:3], 0.0), writes=[("xl", 0)])
                            else:
                                S.op(V, I("tensor_copy", xl[:, 0:3], halo[:, n_, :]), reads=["halo"], writes=[("xl", 0)])
                                S.op(V, I("tensor_copy", xl[:, TC + 3:TC + 6], sconv_t[:, n_, :]),
                                     reads=["sconv_t"], writes=[("xl", 1)])
                            for ti_, (c0, n) in enumerate(NT):
                                b = pbank()
                                for k in range(16):
                                    S.op(PE, I("matmul", PB[b][:, 0:n], lhsT=wx[:, k, ni * 128:(ni + 1) * 128],
                                               rhs=h[:, k, c0:c0 + n], start=(k == 0), stop=(k == 15)),
                                         reads=[("wb", bi), ("h", k)], writes=[("pb", b)], own_sync=False)
                                xo = c0 + 3 if c0 < TC else TC + 6
                                S.op(A, I("copy", xl[:, xo:xo + n], PB[b][:, 0:n]),
                                     reads=[("pb", b)], writes=[("xl", min(ti_, 1))])
                            for ti_, (c0, n) in enumerate(NT):
                                b = pbank()
                                for k in range(16):
                                    S.op(PE, I("matmul", PB[b][:, 0:n], lhsT=wz[:, k, ni * 128:(ni + 1) * 128],
                                               rhs=h[:, k, c0:c0 + n], start=(k == 0), stop=(k == 15)),
                                         reads=[("wb", bi), ("h", k)], writes=[("pb", b)], own_sync=False)
                                S.op(A, I("activation", out=zst[:, c0:c0 + n], in_=PB[b][:, 0:n], func=AF.Silu),
                                     reads=[("pb", b)], writes=[("zst", min(ti_, 1))])
                            cw = lambda tap, n_=n_: vec[:, 64 + tap * 16 + n_: 64 + tap * 16 + n_ + 1]
                            cb = vec[:, 128 + n_:128 + n_ + 1]
                            gt = [[(0, 512)], [(512, 512)] + ([(TC + 3, TS)] if has_s else [])]
                            for sg, (ca, cz) in enumerate(SEG):
                                xlk = [("xl", 0)] if sg == 0 else [("xl", 0), ("xl", 1)]
                                S.op(V, I("tensor_scalar", xc[:, ca:cz], xl[:, ca:cz], cw(0), cb, ALU.mult, ALU.add),
                                     reads=xlk + ["vec"], writes=[("xc", sg)])
                                for tap in range(1, 4):
                                    S.op(V, I("scalar_tensor_tensor", xc[:, ca:cz], xl[:, ca + tap:cz + tap], cw(tap),
                                              xc[:, ca:cz], ALU.mult, ALU.add),
                                         reads=xlk + ["vec", ("xc", sg)], writes=[("xc", sg)])
                                if sg == 1:
                                    if ch == 1:
                                        S.op(V, I("tensor_copy", cvst[:, n_, :], xl[:, TC:TC + 3]),
                                             reads=[("xl", 1)], writes=["cvst"])
                                        S.op(V, I("tensor_copy", cvss[:, n_, :], xl[:, TC + 7:TC + 10]),
                                             reads=[("xl", 1)], writes=["cvss"])
                                    else:
                                        S.op(V, I("tensor_copy", halo[:, n_, :], xl[:, TC:TC + 3]),
                                             reads=[("xl", 1)], writes=["halo"])
                                S.op(V, I("tensor_copy", xcb[:, ca:cz], xc[:, ca:cz]), reads=[("xc", sg)], writes=[("xcb", sg)])
                                for (c0, n) in gt[sg]:
                                    for (wmat, dst, bcol, nm) in ((wrb, rt, 144, "rt"), (wib, it, 160, "it")):
                                        b = pbank()
                                        S.op(PE, I("matmul", PB[b][:, 0:n], lhsT=wmat[:, n_, :], rhs=xcb[:, c0:c0 + n],
                                                   start=True, stop=True),
                                             reads=["wrb", "wib", ("xcb", sg)], writes=[("pb", b)], own_sync=False)
                                        S.op(A, I("activation", out=dst[:, c0:c0 + n], in_=PB[b][:, 0:n], func=AF.Tanh,
                                                  scale=0.5, bias=vech[:, bcol - 144 + n_:bcol - 144 + n_ + 1]),
                                             reads=[("pb", b), "vech"], writes=[(nm, sg)])
                            for sg, (ca, cz) in enumerate(SEG):
                                S.op(V, I("scalar_tensor_tensor", gx[:, ca:cz], it[:, ca:cz], 1.0, xc[:, ca:cz], ALU.add, ALU.mult),
                                     reads=[("it", sg), ("xc", sg)], writes=[("gx", sg)])
                            for sg, (ca, cz) in enumerate(SEG):
                                S.op(A, I("activation", out=a2[:, ca:cz], in_=rt[:, ca:cz], func=AF.Exp,
                                          scale=lsch[:, 16 + n_:16 + n_ + 1], bias=lsch[:, 16 + n_:16 + n_ + 1]),
                                     reads=[("rt", sg), "lsch"], writes=[("a2", sg)])
                                S.op(A, I("activation", out=rt[:, ca:cz], in_=rt[:, ca:cz], func=AF.Exp,
                                          scale=lsch[:, n_:n_ + 1], bias=lsch[:, n_:n_ + 1]),
                                     reads=[("rt", sg), "lsch"], writes=[("rt", sg)])
                            for sg, (ca, cz) in enumerate(SEG):
                                S.op(A, I("activation", out=a2[:, ca:cz], in_=a2[:, ca:cz], func=AF.Relu, scale=-1.0, bias=1.0),
                                     reads=[("a2", sg)], writes=[("a2", sg)])
                            for sg, (ca, cz) in enumerate(SEG):
                                S.op(A, I("activation", out=a2[:, ca:cz], in_=a2[:, ca:cz], func=AF.Sqrt),
                                     reads=[("a2", sg)], writes=[("a2", sg)])
                            for sg, (ca, cz) in enumerate(SEG):
                                S.op(V, I("scalar_tensor_tensor", gx[:, ca:cz], gx[:, ca:cz], 0.5, a2[:, ca:cz], ALU.mult, ALU.mult),
                                     reads=[("gx", sg), ("a2", sg)], writes=[("gx", sg)])
                                if sg == 0:
                                    init = 0.0 if ch == 0 else hcar[:, n_:n_ + 1]
                                    S.op(V, I("tensor_tensor_scan", hst[:, 0:512], rt[:, 0:512], gx[:, 0:512], init,
                                              ALU.mult, ALU.add),
                                         reads=[("rt", 0), ("gx", 0), "hcar"], writes=[("hst", 0)])
                                else:
                                    S.op(V, I("tensor_tensor_scan", hst[:, 512:TC], rt[:, 512:TC], gx[:, 512:TC],
                                              hst[:, 511:512], ALU.mult, ALU.add),
                                         reads=[("rt", 1), ("gx", 1), ("hst", 0)], writes=[("hst", 1)])
                                    if has_s:
                                        S.op(V, I("tensor_tensor_scan", hst[:, TC + 3:TC + 3 + TS], rt[:, TC + 3:TC + 3 + TS],
                                                  gx[:, TC + 3:TC + 3 + TS], sh_t[:, n_:n_ + 1], ALU.mult, ALU.add),
                                             reads=[("rt", 1), ("gx", 1), "sh_t"], writes=[("hst", 1)])
                            S.op(V, I("tensor_copy", hcar[:, n_:n_ + 1], hst[:, TC - 1:TC]),
                                 reads=[("hst", 1)], writes=["hcar"])
                            if ch == 1:
                                S.op(V, I("tensor_copy", hsts[:, n_:n_ + 1], hst[:, TC + 6:TC + 7]),
                                     reads=[("hst", 1)], writes=["hsts"])
                            for ti_, (c0, n) in enumerate(NT):
                                ho = c0 if c0 < TC else TC + 3
                                sg = min(ti_, 1)
                                S.op(V, I("tensor_tensor", ylin[:, n_, c0:c0 + n], hst[:, ho:ho + n], zst[:, c0:c0 + n], ALU.mult),
                                     reads=[("hst", sg), ("zst", sg)], writes=[("ylin", n_)])
                        if np_ == 7 and ch == 1:
                            for (src_t, dst_d, key) in ((cvst, convp[l], "cvst"), (cvss, convs[l], "cvss")):
                                ev = S.dma("sync", I("dma_start", out=dst_d, in_=src_t[:]),
                                           reads=[key])
                                out_events.append(ev)
                            for (src_t, dst_d, key) in ((hcar, hpo[l], "hcar"), (hsts, hso[l], "hsts")):
                                ev = S.dma("sync", I("dma_start", out=dst_d, in_=src_t[:]),
                                           reads=[key])
                                out_events.append(ev)
                    add([(w_in[l][:, XLO + np_ * 256: XLO + np_ * 256 + 256], 16, 256),
                         (w_in[l][:, ZLO + np_ * 256: ZLO + np_ * 256 + 256], 16, 256)], lru_step)
                    if ch == 1 and l + 1 < L:
                        add(*make_ada(l + 1, np_))

                for mp in range(8):
                    def merge1(bi, views, mp=mp, NT=NT):
                        wga, wpa = views
                        if mp == 0:
                            region_barrier()
                        for mi in range(2):
                            m = mp * 2 + mi
                            for (c0, n) in NT:
                                b = pbank()
                                for k in range(16):
                                    S.op(PE, I("matmul",
                                        PB[b][:, 0:n], lhsT=wga[:, k, mi * 128:(mi + 1) * 128],
                                        rhs=h[:, k, c0:c0 + n], start=(k == 0), stop=(k == 15)),
                                        reads=[("wb", bi), ("h", k)], writes=[("pb", b)], own_sync=False)
                                S.op(A, I("activation", out=gat[:, 0:n], in_=PB[b][:, 0:n], func=AF.Sigmoid),
                                     reads=[("pb", b)], writes=["gat"])
                                b2 = 4 + (b % 2)
                                for k in range(8):
                                    S.op(PE, I("matmul",
                                        PB[b2][:, 0:n], lhsT=wpa[:, k, mi * 128:(mi + 1) * 128],
                                        rhs=att[:, k, c0:c0 + n], start=(k == 0), stop=(k == 7)),
                                        reads=[("wb", bi), ("att", k)], writes=[("pb", b2)], own_sync=False)
                                S.op(V, I("tensor_tensor",
                                    merged[:, m, c0:c0 + n], gat[:, 0:n], PB[b2][:, 0:n], ALU.mult),
                                    reads=["gat", ("pb", b2)], writes=[("mg", m)])
                    add([(w_in[l][:, GAO + mp * 256: GAO + mp * 256 + 256], 16, 256),
                         (w_pa[l][:, mp * 256: mp * 256 + 256], 8, 256)], merge1)

                    def merge2(bi, views, mp=mp, NT=NT):
                        wgl, wpb = views
                        for mi in range(2):
                            m = mp * 2 + mi
                            for (c0, n) in NT:
                                b = pbank()
                                for k in range(16):
                                    S.op(PE, I("matmul",
                                        PB[b][:, 0:n], lhsT=wgl[:, k, mi * 128:(mi + 1) * 128],
                                        rhs=h[:, k, c0:c0 + n], start=(k == 0), stop=(k == 15)),
                                        reads=[("wb", bi), ("h", k)], writes=[("pb", b)], own_sync=False)
                                S.op(A, I("activation", out=gat[:, 0:n], in_=PB[b][:, 0:n], func=AF.Sigmoid),
                                     reads=[("pb", b)], writes=["gat"])
                                b2 = 4 + (b % 2)
                                for k in range(16):
                                    S.op(PE, I("matmul",
                                        PB[b2][:, 0:n], lhsT=wpb[:, k, mi * 128:(mi + 1) * 128],
                                        rhs=ylin[:, k, c0:c0 + n], start=(k == 0), stop=(k == 15)),
                                        reads=[("wb", bi), ("ylin", k)], writes=[("pb", b2)], own_sync=False)
                                S.op(V, I("tensor_tensor", gat[:, 0:n], gat[:, 0:n], PB[b2][:, 0:n], ALU.mult),
                                     reads=["gat", ("pb", b2)], writes=["gat"])
                                S.op(V, I("tensor_tensor",
                                    merged[:, m, c0:c0 + n], merged[:, m, c0:c0 + n], gat[:, 0:n], ALU.add),
                                    reads=["gat", ("mg", m)], writes=[("mg", m)])
                    add([(w_in[l][:, GLO + mp * 256: GLO + mp * 256 + 256], 16, 256),
                         (w_pb[l][:, mp * 256: mp * 256 + 256], 16, 256)], merge2)
                    if ch == 1 and l + 1 < L and mp < 4:
                        add(*make_ada(l + 1, 8 + mp))

                for op_ in range(4):
                    def out_step(bi, views, l=l, op_=op_, nt=nt, NT=NT, segs=segs, load_x=load_x, t0=t0, has_s=has_s, mod=mod, MK=MK):
                        wo = views[0]
                        if op_ == 0:
                            region_barrier()
                        load_x(op_ * 4, (op_ * 4) % 2)
                        for mi in range(4):
                            m = op_ * 4 + mi
                            buf = m % 2
                            if mi + 1 < 4:
                                load_x(m + 1, (m + 1) % 2)
                            for (c0, n) in NT:
                                b = pbank()
                                for k in range(16):
                                    S.op(PE, I("matmul",
                                        PB[b][:, 0:n], lhsT=wo[:, k, mi * 128:(mi + 1) * 128],
                                        rhs=merged[:, k, c0:c0 + n], start=(k == 0), stop=(k == 15)),
                                        reads=[("wb", bi), ("mg", k)], writes=[("pb", b)], own_sync=False)
                                j = 0 if c0 < TC else 1
                                S.op(V, I("scalar_tensor_tensor",
                                    xld[buf][:, c0:c0 + n], PB[b][:, 0:n], mod[:, 32 + m, j:j + 1], xld[buf][:, c0:c0 + n],
                                    ALU.mult, ALU.add),
                                    reads=[("pb", b), MK, ("xld", buf)], writes=[("xld", buf)])
                            S.dma("sync", I("dma_start",
                                out=xres[m * 128:(m + 1) * 128, t0:t0 + TC], in_=xld[buf][:, 0:TC]),
                                reads=[("xld", buf)], writes=[("xres", m)])
                            if has_s:
                                S.dma("sync", I("dma_start",
                                    out=xres[m * 128:(m + 1) * 128, T:T + TS], in_=xld[buf][:, TC:TC + TS]),
                                    reads=[("xld", buf)], writes=[("xres", m)])
                    add([(w_out[l][:, op_ * 512:(op_ + 1) * 512], 16, 512)], out_step)

        def final_step(bi, views):
            region_barrier()
            for ch in range(2):
                t0 = ch * TC
                nt = TC + (TS if ch == 1 else 0)
                NT = ntiles(nt)

                def ld(k, buf):
                    S.dma("sync", I("dma_start", out=XR[buf][:, 0:TC], in_=xres[k * 128:(k + 1) * 128, t0:t0 + TC]),
                          reads=[("xres", k)], writes=[XK[buf]])
                    if ch == 1:
                        S.dma("sync", I("dma_start", out=XR[buf][:, TC:nt], in_=xres[k * 128:(k + 1) * 128, T:T + TS]),
                              reads=[("xres", k)], writes=[XK[buf]])
                NR = 6
                base = ch * 32
                for i0 in range(NR - 1):
                    ld(i0 % 16, (base + i0) % NR)
                for k in range(16):
                    i_ = k
                    if i_ + NR - 1 < 32:
                        ld((i_ + NR - 1) % 16, (base + i_ + NR - 1) % NR)
                    xb, xk = XR[(base + i_) % NR], XK[(base + i_) % NR]
                    S.op(A, I("activation", out=sqb[k % 2][:, 0:nt], in_=xb[:, 0:nt], func=AF.Square),
                         reads=[xk], writes=[("sqb", k % 2)])
                    for ti, (c0, n) in enumerate(NT):
                        S.op(PE, I("matmul", PB[4 + ti][:, 0:n], lhsT=ones_b[:], rhs=sqb[k % 2][:, c0:c0 + n],
                                   start=(k == 0), stop=(k == 15)),
                             reads=[("sqb", k % 2), "ones_b"], writes=[("pb", 4 + ti)], own_sync=False)
                for ti, (c0, n) in enumerate(NT):
                    S.op(V, I("tensor_scalar", rstd[:, c0:c0 + n], PB[4 + ti][:, 0:n], 1.0 / D, 1e-6, ALU.mult, ALU.add),
                         reads=[("pb", 4 + ti)], writes=["rstd"])
                S.op(A, I("activation", out=rstd[:, 0:nt], in_=rstd[:, 0:nt], func=AF.Sqrt),
                     reads=["rstd"], writes=["rstd"])
                S.op(V, I("reciprocal", rstd[:, 0:nt], rstd[:, 0:nt]),
                     reads=["rstd"], writes=["rstd"])
                for k in range(16):
                    i_ = 16 + k
                    if i_ + NR - 1 < 32:
                        ld((i_ + NR - 1) % 16, (base + i_ + NR - 1) % NR)
                    xb, xk = XR[(base + i_) % NR], XK[(base + i_) % NR]
                    S.op(V, I("scalar_tensor_tensor", xb[:, 0:nt], xb[:, 0:nt], fg[:, k:k + 1], rstd[:, 0:nt],
                              ALU.mult, ALU.mult),
                         reads=[xk, "rstd", "fg"], writes=[xk])
                    ev = S.dma("sync", I("dma_start", out=yT[k * 128:(k + 1) * 128, t0:t0 + TC], in_=xb[:, 0:TC]),
                               reads=[xk])
                    out_events.append(ev)
                    if ch == 1:
                        ev = S.dma("sync", I("dma_start", out=yT[k * 128:(k + 1) * 128, T:T + TS], in_=xb[:, TC:nt]),
                                   reads=[xk])
                        out_events.append(ev)
        add(None, final_step)

        wsteps = [i for i, (p, f) in enumerate(steps) if p is not None]
        loaded = {}
        nxt = 0
        import os as _os
        _maxs = int(_os.environ.get('MAXSTEPS', '100000'))
        steps = steps[:_maxs]
        wsteps = [j for j in wsteps if j < _maxs]
        for i, (p, f) in enumerate(steps):
            upcoming = [j for j in wsteps[nxt:nxt + NWB]]
            ahead = [j for j in wsteps if j >= i][:NWB]
            for j in ahead:
                if j not in loaded and j >= (wsteps[nxt] if nxt < len(wsteps) else 1 << 30):
                    loaded[j] = wload(steps[j][0])
                    nxt += 1
            if _STEP_HOOK is not None:
                _STEP_HOOK(i, f.__name__)
            if p is None:
                f(None, None)
            else:
                bi, views = loaded.pop(i)
                f(bi, views)
        S.finish(out_events)
    return nc


_CACHE = {}


def _fm(v):
    v = np.asarray(v, np.float32)
    return np.ascontiguousarray(v.reshape(v.shape[:-1] + (16, 128)).swapaxes(-1, -2))


def kernel(x_prompt, x_sample, c_prompt, c_sample, cache_kv_g0, cache_kv_g1, cache_kv_g2,
           state_conv, state_h, rel_bias, w_ada, b_ada, norm_g, w_in, conv_w, conv_b,
           w_r, b_r, w_i, b_i, lam, w_pa, w_pb, w_out, final_g, _depth=DEPTH):
    L = _depth
    if L not in _CACHE:
        _CACHE[L] = build(L)
    nc = _CACHE[L]
    bP, bC, bN = _gather_bias(rel_bias)
    f = lambda a: np.ascontiguousarray(np.asarray(a, np.float32))
    vecs = np.concatenate([
        np.asarray(b_ada, np.float32)[:L].reshape(L, 48, 128).swapaxes(1, 2),
        _fm(norm_g[:L]),
        _fm(conv_w[:L]).transpose(0, 2, 1, 3).reshape(L, 128, 64),
        _fm(conv_b[:L]), _fm(b_r[:L]), _fm(b_i[:L]), _fm(lam[:L])], axis=2)
    vecs = f(vecs)
    shared = {
        "biasP": f(bP.reshape(4, 128, -1)), "biasC": f(bC.reshape(4, 128, -1)), "biasN": f(bN.reshape(4, 4, -1)),
        "w_ada": f(w_ada[:L]), "w_in": f(w_in[:L]), "w_pa": f(w_pa[:L]), "w_pb": f(w_pb[:L]),
        "w_out": f(w_out[:L]), "w_r": f(w_r[:L]), "w_i": f(w_i[:L]), "vecs": vecs, "fing": _fm(final_g), "identin": np.eye(128, dtype=np.float32),
    }
    caches = (cache_kv_g0, cache_kv_g1, cache_kv_g2)
    in_maps = []
    for c in range(8):
        b = c % 4
        m = dict(shared)
        m["xin"] = f(np.concatenate([np.asarray(x_prompt[b], np.float32).T, np.asarray(x_sample[c], np.float32).T], axis=1))
        m["cvec"] = f(np.stack([_fm(c_prompt[b]), _fm(c_sample[c])], axis=-1))
        for g in range(3):
            m["cache%d" % g] = f(np.asarray(caches[g])[:L, c])
        m["sconv"] = f(_fm(np.asarray(state_conv)[:L, c]).transpose(0, 2, 3, 1))
        m["sh"] = _fm(np.asarray(state_h)[:L, c])
        in_maps.append(m)
    res = run_bass_kernel_spmd(nc, in_maps, core_ids=list(range(8)))
    R = res.results

    def unfm(a):
        return np.ascontiguousarray(np.swapaxes(a, -1, -2).reshape(a.shape[:-2] + (2048,)))
    y_prompt = np.stack([R[b]["yT"][:, :T].T for b in range(4)])
    y_sample = np.stack([R[c]["yT"][:, T:].T for c in range(8)])
    kvp = [np.stack([R[b]["kvp%d" % g] for b in range(4)], axis=1) for g in range(3)]
    kvs = [np.stack([R[c]["kvs%d" % g] for c in range(8)], axis=1) for g in range(3)]
    convp = np.stack([unfm(R[b]["convp"].transpose(0, 3, 1, 2)) for b in range(4)], axis=1)
    convs = np.stack([unfm(R[c]["convs"].transpose(0, 3, 1, 2)) for c in range(8)], axis=1)
    hp_ = np.stack([unfm(R[b]["hpo"]) for b in range(4)], axis=1)
    hs_ = np.stack([unfm(R[c]["hso"]) for c in range(8)], axis=1)
    outs = (y_prompt, y_sample, kvp[0], kvp[1], kvp[2], convp, hp_, kvs[0], kvs[1], kvs[2], convs, hs_)
    return tuple(np.ascontiguousarray(o, dtype=np.float32) for o in outs)
```
